# Optimizing a Trainium2 kernel written in Bass

```python
import jax
import jax.numpy as jnp
from jax import lax
import numpy as np

D_MODEL = 2048
BATCH = 4
SEQ = 4096
DEPTH = 4

GRID_W = 64
CTX_LEN = 256
NORM_EPS = 1e-6
NEG_INF = -1e30
N_MOD = 6

NA_HEADS = 8
NA_HEAD_DIM = 128
NA_WIDTH = NA_HEADS * NA_HEAD_DIM
NA_WIN_H = 8
NA_WIN_W = 16

LRU_WIDTH = 1024
LRU_BLOCKS = 8
LRU_BLOCK_DIM = LRU_WIDTH // LRU_BLOCKS
LRU_CONV_W = 4
LRU_C = 8.0

MLA_HEADS = 8
MLA_Q_RANK = 512
MLA_KV_RANK = 256
MLA_NOPE_DIM = 128
MLA_ROPE_DIM = 64
MLA_V_DIM = 128
MLA_QK_DIM = MLA_NOPE_DIM + MLA_ROPE_DIM
MLA_WIDTH = MLA_HEADS * MLA_V_DIM
ROPE_THETA = 10000.0
Q_BLOCK = 128

N_BRANCH = 3
BRANCH_WIDTH = 1024
D_IN = 3 * NA_WIDTH + 2 * LRU_WIDTH + MLA_Q_RANK + MLA_KV_RANK + MLA_ROPE_DIM + N_BRANCH * D_MODEL

D_FF = 5632
FFN_CONV_W = 3

kernel_name = 'hybrid_natten_rglru_mla_prefix_dit_block'


def rmsnorm(x, g):
    xf = x.astype(jnp.float32)
    y = xf * lax.rsqrt(jnp.mean(xf * xf, axis=-1, keepdims=True) + NORM_EPS)
    return (y * g.astype(jnp.float32)).astype(x.dtype)


def modulate(h, shift, scale):
    return h * (1 + scale) + shift


def depthwise_conv(x, w, b):
    width = w.shape[0]
    n = x.shape[1]
    pad_left = width // 2
    xp = jnp.pad(x, ((0, 0), (pad_left, width - 1 - pad_left), (0, 0)))
    return sum(xp[:, i:i + n] * w[i] for i in range(width)) + b


def axial_rope_angles(n_tok, dim):
    t = jnp.arange(n_tok, dtype=jnp.int32)
    row = (t // GRID_W).astype(jnp.float32)
    col = (t % GRID_W).astype(jnp.float32)
    n_freq = dim // 4
    inv_freq = ROPE_THETA ** (-jnp.arange(n_freq, dtype=jnp.float32) / n_freq)
    ang = jnp.concatenate([row[:, None] * inv_freq, col[:, None] * inv_freq], axis=-1)
    return jnp.cos(ang), jnp.sin(ang)


def apply_rope(x, cos, sin):
    half = x.shape[-1] // 2
    xf = x.astype(jnp.float32)
    x1, x2 = xf[..., :half], xf[..., half:]
    return jnp.concatenate([x1 * cos - x2 * sin, x1 * sin + x2 * cos], axis=-1).astype(x.dtype)


def split_in_proj(z):
    sizes = (3 * NA_WIDTH, LRU_WIDTH, LRU_WIDTH, MLA_Q_RANK, MLA_KV_RANK, MLA_ROPE_DIM)
    offsets = [int(o) for o in np.cumsum(sizes)]
    return jnp.split(z, offsets, axis=-1)


def softmax_attention(q, k, v):
    scale = q.shape[-1] ** -0.5
    s = jnp.einsum('bqhd,bkhd->bhqk', q, k).astype(jnp.float32) * scale
    p = jax.nn.softmax(s, axis=-1).astype(v.dtype)
    o = jnp.einsum('bhqk,bkhd->bqhd', p, v)
    return o.reshape(o.shape[0], o.shape[1], -1)


def neighbourhood_attention(q, k, v, k_ctx, v_ctx, rpb):
    bsz, n, heads, hd = q.shape
    rows = n // GRID_W
    kh = min(NA_WIN_H, rows)
    r = np.arange(rows)
    key_rows = np.clip(r - kh // 2, 0, rows - kh)[:, None] + np.arange(kh)[None, :]
    row_idx = key_rows - r[:, None] + (NA_WIN_H - 1)
    cidx = np.arange(GRID_W)
    c_start = np.clip(cidx - NA_WIN_W // 2, 0, GRID_W - NA_WIN_W)
    in_win = (cidx[None, :] >= c_start[:, None]) & (cidx[None, :] < c_start[:, None] + NA_WIN_W)
    col_idx = np.clip(cidx[None, :] - cidx[:, None], -(NA_WIN_W - 1), NA_WIN_W - 1) + (NA_WIN_W - 1)
    bias = rpb.astype(jnp.float32)[:, row_idx][..., col_idx]
    bias = jnp.where(in_win[None, None, :, None, :], bias.transpose(0, 1, 3, 2, 4), NEG_INF)
    scale = hd ** -0.5
    qg = q.reshape(bsz, rows, GRID_W, heads, hd)
    kg = jnp.take(k.reshape(bsz, rows, GRID_W, heads, hd), key_rows, axis=1)
    vg = jnp.take(v.reshape(bsz, rows, GRID_W, heads, hd), key_rows, axis=1)
    s_win = jnp.einsum('brqhd,brjkhd->bhrqjk', qg, kg).astype(jnp.float32) * scale + bias
    s_ctx = jnp.einsum('brqhd,blhd->bhrql', qg, k_ctx).astype(jnp.float32) * scale
    n_win = kh * GRID_W
    s = jnp.concatenate([s_win.reshape(bsz, heads, rows, GRID_W, n_win), s_ctx], axis=-1)
    p = jax.nn.softmax(s, axis=-1).astype(v.dtype)
    p_win = p[..., :n_win].reshape(bsz, heads, rows, GRID_W, kh, GRID_W)
    o = jnp.einsum('bhrqjk,brjkhd->brqhd', p_win, vg) + jnp.einsum('bhrql,blhd->brqhd', p[..., n_win:], v_ctx)
    return o.reshape(bsz, n, heads * hd)


def rglru_coeffs(x, w_a, b_a, w_x, b_x, lam):
    bsz, n, _ = x.shape
    xb = x.reshape(bsz, n, LRU_BLOCKS, LRU_BLOCK_DIM)
    gate_a = jnp.einsum('bnkc,kcd->bnkd', xb, w_a).reshape(bsz, n, LRU_WIDTH) + b_a
    gate_x = jnp.einsum('bnkc,kcd->bnkd', xb, w_x).reshape(bsz, n, LRU_WIDTH) + b_x
    r = jax.nn.sigmoid(gate_a.astype(jnp.float32))
    i = jax.nn.sigmoid(gate_x.astype(jnp.float32))
    log_a = -LRU_C * r * jax.nn.softplus(-lam.astype(jnp.float32))
    a = jnp.exp(log_a)
    b = jnp.sqrt(-jnp.expm1(2.0 * log_a)) * (i * x.astype(jnp.float32))
    return a, b


def _combine(e1, e2):
    a1, b1 = e1
    a2, b2 = e2
    return a1 * a2, a2 * b1 + b2


def linear_recurrence(a, b, h0):
    b = b.at[:, 0].add(a[:, 0] * h0)
    _, h = lax.associative_scan(_combine, (a, b), axis=1)
    return h


def bidirectional_rglru(u, u_c, w_a, b_a, w_x, b_x, lam):
    h0 = jnp.zeros((u_c.shape[0], LRU_WIDTH), jnp.float32)
    a_c, b_c = rglru_coeffs(u_c, w_a[0], b_a[0], w_x[0], b_x[0], lam[0])
    a_l, b_l = rglru_coeffs(u, w_a[0], b_a[0], w_x[0], b_x[0], lam[0])
    h_cf = linear_recurrence(a_c, b_c, h0)
    h_lf = linear_recurrence(a_l, b_l, h_cf[:, -1])
    a_c, b_c = rglru_coeffs(jnp.flip(u_c, 1), w_a[1], b_a[1], w_x[1], b_x[1], lam[1])
    a_l, b_l = rglru_coeffs(jnp.flip(u, 1), w_a[1], b_a[1], w_x[1], b_x[1], lam[1])
    h_cb = linear_recurrence(a_c, b_c, h0)
    h_lb = linear_recurrence(a_l, b_l, h_cb[:, -1])
    y = (h_lf + jnp.flip(h_lb, 1)).astype(u.dtype)
    return y, h_cf, jnp.flip(h_cb, 1)


def mla_queries(cq, q_norm, w_q_up, cos, sin):
    bsz, n, _ = cq.shape
    q = (rmsnorm(cq, q_norm) @ w_q_up).reshape(bsz, n, MLA_HEADS, MLA_QK_DIM)
    q_nope, q_rope = q[..., :MLA_NOPE_DIM], q[..., MLA_NOPE_DIM:]
    if cos is not None:
        q_rope = apply_rope(q_rope, cos[:, None, :], sin[:, None, :])
    return jnp.concatenate([q_nope, q_rope], axis=-1)


def mla_keys_values(ckv, k_rope, kv_norm, w_kv_up, cos, sin):
    bsz, n, _ = ckv.shape
    kv = (rmsnorm(ckv, kv_norm) @ w_kv_up).reshape(bsz, n, MLA_HEADS, MLA_NOPE_DIM + MLA_V_DIM)
    k_nope, v = kv[..., :MLA_NOPE_DIM], kv[..., MLA_NOPE_DIM:]
    if cos is not None:
        k_rope = apply_rope(k_rope, cos, sin)
    k_rope = jnp.broadcast_to(k_rope[:, :, None, :], (bsz, n, MLA_HEADS, MLA_ROPE_DIM))
    return jnp.concatenate([k_nope, k_rope], axis=-1), v


def blockwise_attention(q, k, v, k_ctx, v_ctx):
    bsz, n, heads, dq = q.shape
    scale = dq ** -0.5
    kk = jnp.concatenate([k, k_ctx], axis=1)
    vv = jnp.concatenate([v, v_ctx], axis=1)
    qb = q.reshape(bsz, n // Q_BLOCK, Q_BLOCK, heads, dq).transpose(1, 0, 2, 3, 4)

    def attend(qi):
        s = jnp.einsum('bqhd,bkhd->bhqk', qi, kk).astype(jnp.float32) * scale
        p = jax.nn.softmax(s, axis=-1).astype(vv.dtype)
        return jnp.einsum('bhqk,bkhd->bqhd', p, vv)

    o = lax.map(attend, qb)
    return o.transpose(1, 0, 2, 3, 4).reshape(bsz, n, heads * vv.shape[-1])


def merge_branches(branches, gate_logits, w_branch, w_out):
    bsz, n, _ = gate_logits.shape
    gates = jax.nn.sigmoid(gate_logits).reshape(bsz, n, N_BRANCH, D_MODEL)
    y = sum(gates[:, :, i] * (branches[i] @ w_branch[i]) for i in range(N_BRANCH))
    return y @ w_out


def mixing_sublayer(h, hc, cos, sin, w_in, na_rpb, lru_conv_w, lru_conv_b, lru_w_a, lru_b_a, lru_w_x, lru_b_x,
                    lru_lam, mla_q_norm, mla_kv_norm, mla_w_q_up, mla_w_kv_up, w_branch, w_out, with_ctx_out):
    bsz, n, _ = h.shape
    n_ctx = hc.shape[1]
    na_qkv, lru_x, lru_g, cq, ckv, kr, gate_logits = split_in_proj(h @ w_in)
    na_qkv_c, lru_x_c, lru_g_c, cq_c, ckv_c, kr_c, gate_logits_c = split_in_proj(hc @ w_in)
    qkv = na_qkv.reshape(bsz, n, 3, NA_HEADS, NA_HEAD_DIM)
    qkv_c = na_qkv_c.reshape(bsz, n_ctx, 3, NA_HEADS, NA_HEAD_DIM)
    out_a = neighbourhood_attention(qkv[:, :, 0], qkv[:, :, 1], qkv[:, :, 2], qkv_c[:, :, 1], qkv_c[:, :, 2], na_rpb)
    u = depthwise_conv(lru_x, lru_conv_w, lru_conv_b)
    u_c = depthwise_conv(lru_x_c, lru_conv_w, lru_conv_b)
    y_b, h_cf, h_cb = bidirectional_rglru(u, u_c, lru_w_a, lru_b_a, lru_w_x, lru_b_x, lru_lam)
    out_b = jax.nn.gelu(lru_g) * y_b
    q_m = mla_queries(cq, mla_q_norm, mla_w_q_up, cos, sin)
    k_m, v_m = mla_keys_values(ckv, kr, mla_kv_norm, mla_w_kv_up, cos, sin)
    k_mc, v_mc = mla_keys_values(ckv_c, kr_c, mla_kv_norm, mla_w_kv_up, None, None)
    out_c = blockwise_attention(q_m, k_m, v_m, k_mc, v_mc)
    y = merge_branches((out_a, out_b, out_c), gate_logits, w_branch, w_out)
    if not with_ctx_out:
        return y, None
    out_ac = softmax_attention(qkv_c[:, :, 0], qkv_c[:, :, 1], qkv_c[:, :, 2])
    out_bc = jax.nn.gelu(lru_g_c) * (h_cf + h_cb).astype(lru_g_c.dtype)
    q_mc = mla_queries(cq_c, mla_q_norm, mla_w_q_up, None, None)
    out_cc = softmax_attention(q_mc, k_mc, v_mc)
    yc = merge_branches((out_ac, out_bc, out_cc), gate_logits_c, w_branch, w_out)
    return y, yc


def conv_ffn(h, w_up, conv_w, conv_b, w_down):
    u = depthwise_conv(h @ w_up, conv_w, conv_b)
    val, gate = jnp.split(u, 2, axis=-1)
    return (jax.nn.silu(gate) * val) @ w_down


def setup_inputs(seed: int = 0) -> dict:
    key = jax.random.key(seed)
    ks = jax.random.split(key, 32)
    f32 = jnp.float32
    D = D_MODEL

    def nrm(k, shape, scale):
        return jax.random.normal(k, shape, f32) * scale

    lam_u = jax.random.uniform(ks[16], (DEPTH, 2, LRU_WIDTH), f32, 0.9, 0.999)
    lam_s = lam_u ** (1.0 / LRU_C)
    return {
        'x': nrm(ks[0], (BATCH, SEQ, D), 1.0),
        'c': nrm(ks[1], (BATCH, D), 1.0),
        'ctx': nrm(ks[2], (BATCH, CTX_LEN, D), 1.0),
        'c_ctx': nrm(ks[3], (D,), 1.0),
        'w_mod': nrm(ks[4], (DEPTH, D, N_MOD * D), 0.5 * D ** -0.5),
        'b_mod': nrm(ks[5], (DEPTH, N_MOD * D), 0.02),
        'norm_mix': 1.0 + nrm(ks[6], (DEPTH, D), 0.05),
        'norm_ffn': 1.0 + nrm(ks[7], (DEPTH, D), 0.05),
        'w_in': nrm(ks[8], (DEPTH, D, D_IN), D ** -0.5),
        'na_rpb': nrm(ks[9], (DEPTH, NA_HEADS, 2 * NA_WIN_H - 1, 2 * NA_WIN_W - 1), 0.1),
        'lru_conv_w': nrm(ks[10], (DEPTH, LRU_CONV_W, LRU_WIDTH), LRU_CONV_W ** -0.5),
        'lru_conv_b': nrm(ks[11], (DEPTH, LRU_WIDTH), 0.02),
        'lru_w_a': nrm(ks[12], (DEPTH, 2, LRU_BLOCKS, LRU_BLOCK_DIM, LRU_BLOCK_DIM), LRU_BLOCK_DIM ** -0.5),
        'lru_b_a': nrm(ks[13], (DEPTH, 2, LRU_WIDTH), 0.02),
        'lru_w_x': nrm(ks[14], (DEPTH, 2, LRU_BLOCKS, LRU_BLOCK_DIM, LRU_BLOCK_DIM), LRU_BLOCK_DIM ** -0.5),
        'lru_b_x': nrm(ks[15], (DEPTH, 2, LRU_WIDTH), 0.02),
        'lru_lam': jnp.log(lam_s) - jnp.log1p(-lam_s),
        'mla_q_norm': 1.0 + nrm(ks[17], (DEPTH, MLA_Q_RANK), 0.05),
        'mla_kv_norm': 1.0 + nrm(ks[18], (DEPTH, MLA_KV_RANK), 0.05),
        'mla_w_q_up': nrm(ks[19], (DEPTH, MLA_Q_RANK, MLA_HEADS * MLA_QK_DIM), MLA_Q_RANK ** -0.5),
        'mla_w_kv_up': nrm(ks[20], (DEPTH, MLA_KV_RANK, MLA_HEADS * (MLA_NOPE_DIM + MLA_V_DIM)), MLA_KV_RANK ** -0.5),
        'w_branch': nrm(ks[21], (DEPTH, N_BRANCH, BRANCH_WIDTH, D), BRANCH_WIDTH ** -0.5),
        'w_out': nrm(ks[22], (DEPTH, D, D), D ** -0.5),
        'ffn_w_up': nrm(ks[23], (DEPTH, D, 2 * D_FF), D ** -0.5),
        'ffn_conv_w': nrm(ks[24], (DEPTH, FFN_CONV_W, 2 * D_FF), FFN_CONV_W ** -0.5),
        'ffn_conv_b': nrm(ks[25], (DEPTH, 2 * D_FF), 0.02),
        'ffn_w_down': nrm(ks[26], (DEPTH, D_FF, D), D_FF ** -0.5),
        'norm_final': 1.0 + nrm(ks[27], (D,), 0.05),
    }


def reference(x, c, ctx, c_ctx, w_mod, b_mod, norm_mix, norm_ffn, w_in, na_rpb, lru_conv_w, lru_conv_b,
              lru_w_a, lru_b_a, lru_w_x, lru_b_x, lru_lam, mla_q_norm, mla_kv_norm, mla_w_q_up, mla_w_kv_up,
              w_branch, w_out, ffn_w_up, ffn_conv_w, ffn_conv_b, ffn_w_down, norm_final):
    n = x.shape[1]
    cos, sin = axial_rope_angles(n, MLA_ROPE_DIM)
    silu_c = jax.nn.silu(c)
    silu_cc = jax.nn.silu(c_ctx)
    xc = ctx
    for l in range(DEPTH):
        last = l == DEPTH - 1
        mod = (silu_c @ w_mod[l] + b_mod[l])[:, None, :]
        mod_c = silu_cc @ w_mod[l] + b_mod[l]
        sh1, sc1, g1, sh2, sc2, g2 = jnp.split(mod, N_MOD, axis=-1)
        sh1c, sc1c, g1c, sh2c, sc2c, g2c = jnp.split(mod_c, N_MOD, axis=-1)
        h = modulate(rmsnorm(x, norm_mix[l]), sh1, sc1)
        hc = modulate(rmsnorm(xc, norm_mix[l]), sh1c, sc1c)
        y, yc = mixing_sublayer(h, hc, cos, sin, w_in[l], na_rpb[l], lru_conv_w[l], lru_conv_b[l], lru_w_a[l],
                                lru_b_a[l], lru_w_x[l], lru_b_x[l], lru_lam[l], mla_q_norm[l], mla_kv_norm[l],
                                mla_w_q_up[l], mla_w_kv_up[l], w_branch[l], w_out[l], not last)
        x = x + g1 * y
        h2 = modulate(rmsnorm(x, norm_ffn[l]), sh2, sc2)
        x = x + g2 * conv_ffn(h2, ffn_w_up[l], ffn_conv_w[l], ffn_conv_b[l], ffn_w_down[l])
        if not last:
            xc = xc + g1c * yc
            h2c = modulate(rmsnorm(xc, norm_ffn[l]), sh2c, sc2c)
            xc = xc + g2c * conv_ffn(h2c, ffn_w_up[l], ffn_conv_w[l], ffn_conv_b[l], ffn_w_down[l])
    return rmsnorm(x, norm_final)
```

```python
import numpy as np
import ml_dtypes
from contextlib import ExitStack, contextmanager
import concourse.bass as bass
import concourse.mybir as mybir
from concourse.bass_utils import run_bass_kernel_spmd

F32 = mybir.dt.float32
BF16 = mybir.dt.bfloat16
AF = mybir.ActivationFunctionType
ALU = mybir.AluOpType
AX = mybir.AxisListType
NPBF = ml_dtypes.bfloat16


class Cfg:
    D = 2048; NB = 4; SEQ = 4096; GW = 64; NCTX = 256; DEPTH = 4
    NAH = 8; HD = 128; WINH = 8; WINW = 16
    LRUW = 1024; LRUB = 8
    QR = 512; KVR = 256; NOPE = 128; ROPE = 64; VD = 128; MH = 8
    BW = 1024; DFF = 5632
    EPS = 1e-6; THETA = 10000.0; LRUC = 8.0
    JG = 11

    def __init__(self, **kw):
        for k, v in kw.items():
            setattr(self, k, v)
        self.KD = self.D // 128
        self.NAW = self.NAH * self.HD
        self.NOWN = self.SEQ // 2
        self.NT = self.NCTX + self.NOWN
        self.NC = self.NCTX + 2 + self.NOWN + 1
        self.OWN0 = self.NCTX + 2
        self.NTM = self.NCTX + self.SEQ
        self.DIN = 3 * self.NAW + 2 * self.LRUW + self.QR + self.KVR + self.ROPE + 3 * self.D
        self.GOFF = self.DIN - 3 * self.D
        self.QKD = self.NOPE + self.ROPE
        self.FCH = self.DFF // 128
        self.ROWS = self.SEQ // self.GW


class Buf:
    __slots__ = ("t", "wr", "rd", "sems", "name")

    def __init__(self, t, name=""):
        self.t = t
        self.wr = {}
        self.rd = {}
        self.sems = {}
        self.name = name

    def __getitem__(self, idx):
        return self.t[idx]


class Sched:
    def __init__(self, nc, es):
        self.nc = nc
        self.es = es
        self.engs = {"pe": nc.tensor, "act": nc.scalar, "dve": nc.vector, "pool": nc.gpsimd, "sp": nc.sync}
        self.sem = {}
        self.cnt = {}
        for e in self.engs:
            self.sem[e] = es.enter_context(nc.semaphore("S_" + e))
            self.cnt[e] = 0
        self.waited = {e: {} for e in self.engs}
        self.sempool = {"sw": [], "hw": []}
        self.allsems = {}
        self.nsem = 0
        self.ninst = 0
        self.scopes = []
        self.psbanks = None
        self.psi = 0

    def sb(self, name, shape, dt):
        b = Buf(self.scopes[-1][0].enter_context(self.nc.sbuf_tensor(name + "_%d" % self.ninst, list(shape), dt)), name)
        self.scopes[-1][1].append(b)
        return b

    def dram(self, ap, name=""):
        return Buf(ap, name)

    def init_psum(self, es):
        self.psbanks = [Buf(es.enter_context(self.nc.psum_tensor("psb%d" % i, [128, 512], F32)), "ps%d" % i)
                        for i in range(8)]

    def psum(self, avoid=()):
        while True:
            b = self.psbanks[self.psi % 8]
            self.psi += 1
            if b not in avoid:
                return b

    @contextmanager
    def scope(self):
        es = ExitStack()
        self.scopes.append((es, []))
        try:
            yield
        finally:
            self.barrier()
            _, bufs = self.scopes.pop()
            for b in bufs:
                for qc, ent in b.sems.items():
                    self.allsems[ent[0]][1] = ent[2]
                    self.sempool[qc].append(ent[0])
                b.sems = {}
            es.close()

    def _bufsem(self, b, qc):
        if qc not in b.sems:
            if self.sempool[qc]:
                k = self.sempool[qc].pop()
            else:
                k = "D%d" % self.nsem
                self.nsem += 1
                self.allsems[k] = [self.es.enter_context(self.nc.semaphore(k)), 0]
            b.sems[qc] = [k, self.allsems[k][0], self.allsems[k][1]]
        return b.sems[qc]

    @staticmethod
    def _merge(deps, d):
        for s, v in d.items():
            if deps.get(s, (None, 0))[1] < v[1]:
                deps[s] = v

    def _deps(self, reads, writes, partial=()):
        deps = {}
        for b in reads:
            self._merge(deps, b.wr)
        for b in writes:
            self._merge(deps, b.wr)
            self._merge(deps, b.rd)
        for b in partial:
            self._merge(deps, b.rd)
        return deps

    def _wait(self, e, deps):
        eng = self.engs[e]
        w = self.waited[e]
        own = "E_" + e
        for s, (sem, val) in deps.items():
            if s == own and (e == "pe" or val > self.cnt[e]):
                continue
            if w.get(s, 0) < val:
                eng.wait_ge(sem, val)
                w[s] = val
                self.ninst += 1

    def op(self, e, fn, reads=(), writes=(), inc=True):
        deps = self._deps(reads, writes)
        self._wait(e, deps)
        ins = fn(self.engs[e])
        self.ninst += 1
        key = "E_" + e
        if inc:
            self.cnt[e] += 1
            ins.then_inc(self.sem[e], 1)
            val = self.cnt[e]
        else:
            val = self.cnt[e] + 1
        tok = (self.sem[e], val)
        for b in reads:
            b.rd[key] = tok
        for b in writes:
            b.wr = {key: tok}
            b.rd = {}
        return ins

    def dma(self, q, outs, out_ap, ins, in_ap, semb, partial=False, **kw):
        deps = self._deps(ins, [] if partial else outs, partial=outs if partial else ())
        self._wait(q, deps)
        ent = self._bufsem(semb, "sw" if q == "pool" else "hw")
        ent[2] += 16
        sem = ent[1]
        self.engs[q].dma_start(out=out_ap, in_=in_ap, **kw).then_inc(sem, 16)
        self.ninst += 1
        key = "D_" + ent[0]
        tok = (sem, ent[2])
        for b in ins:
            b.rd[key] = tok
        for b in outs:
            if partial:
                b.wr[key] = tok
            else:
                b.wr = {key: tok}
                b.rd = {}

    def load(self, q, sb_b, sb_ap, dr_bs, dr_ap, part=False, **kw):
        if isinstance(dr_bs, Buf):
            dr_bs = [dr_bs]
        self.dma(q, [sb_b], sb_ap, dr_bs, dr_ap, sb_b, partial=part, **kw)

    def store(self, q, dr_bs, dr_ap, sb_b, sb_ap):
        if isinstance(dr_bs, Buf):
            dr_bs = [dr_bs]
        self.dma(q, dr_bs, dr_ap, [sb_b], sb_ap, sb_b, partial=True)

    def barrier(self):
        deps = {}
        for e in self.engs:
            if self.cnt[e] > 0:
                deps["E_" + e] = (self.sem[e], self.cnt[e])
        for _, bufs in self.scopes:
            for b in bufs:
                for ent in b.sems.values():
                    if ent[2] > 0:
                        deps["D_" + ent[0]] = (ent[1], ent[2])
        for e in self.engs:
            self._wait(e, deps)


def blocks(lo, hi, w=512):
    return [(a, min(a + w, hi)) for a in range(lo, hi, w)]


def bcast_mid(ap2d, n):
    a = ap2d.ap
    return bass.AP(ap2d.tensor, ap2d.offset, [list(a[0]), [0, n], list(a[-1])])


def rev(ap2d):
    a = ap2d.ap
    n = a[-1][1]
    return bass.AP(ap2d.tensor, ap2d.offset + (n - 1) * a[-1][0], [list(a[0]), [-a[-1][0], n]])


def r_io(C, merge, inproj, final):
    ins, outs = {}, {}
    KD = C.KD
    ins["xT"] = ([C.D, C.NC], F32)
    ins["cflag"] = ([128, 3], F32)
    if merge:
        ins["modp"] = ([128, 6 * KD, 2], F32)
        for n in ("AT", "BT", "CT"):
            ins[n] = ([C.BW, C.NC], BF16)
        ins["nmix_p"] = ([128, KD], F32)
        ins["nffn_p"] = ([128, KD], F32)
        ins["win_p"] = ([C.D, C.DIN], F32)
        ins["wbr_p"] = ([3, C.BW, C.D], F32)
        ins["wout_p"] = ([C.D, C.D], F32)
        ins["wup_p"] = ([C.D, 2 * C.DFF], F32)
        ins["cw_p"] = ([128, 3, 2 * C.FCH], F32)
        ins["cb_p"] = ([128, 2 * C.FCH], F32)
        ins["wdn_p"] = ([C.DFF, C.D], F32)
        outs["xTo"] = ([C.D, C.NC], F32)
        outs["h2s"] = ([C.D, C.NC], BF16)
    if inproj:
        ins["cT"] = ([128, KD, 2], F32)
        ins["wmod"] = ([C.D, 6 * C.D], F32)
        ins["bmod"] = ([128, 6 * KD], F32)
        ins["nmix"] = ([128, KD], F32)
        ins["win"] = ([C.D, C.DIN], F32)
        ins["qn"] = ([128, C.QR // 128], F32)
        ins["kvn"] = ([128, C.KVR // 128], F32)
        ins["cos2"] = ([C.ROPE, C.NOWN], F32)
        ins["sin2"] = ([C.ROPE, C.NOWN], F32)
        ins["rotm"] = ([C.ROPE, C.ROPE], F32)
        outs["modo"] = ([128, 6 * KD, 2], F32)
        outs["QT"] = ([C.NAW, C.NT], BF16)
        outs["KT"] = ([C.NAW, C.NT], BF16)
        outs["V"] = ([C.NT, C.NAW], BF16)
        outs["LXT"] = ([C.LRUW, C.NT], F32)
        outs["GGT"] = ([C.LRUW, C.NT], BF16)
        outs["CQN"] = ([C.QR, C.NT], BF16)
        outs["CKVN"] = ([C.KVR, C.NT], BF16)
        outs["KRT"] = ([C.ROPE, C.NT], BF16)
    if final:
        ins["nfin"] = ([128, KD], F32)
        outs["oT"] = ([C.D, C.NOWN], F32)
    return ins, outs


def declare(nc, S, ins, outs, scratch_internal=False):
    T = {}
    for n, (shape, dt) in ins.items():
        T[n] = S.dram(nc.dram_tensor(n, shape, dt, kind="ExternalInput").ap(), n)
    for n, (shape, dt) in outs.items():
        T[n] = S.dram(nc.dram_tensor(n, shape, dt, kind="ExternalOutput").ap(), n)
    return T


class Grid:
    def __init__(self, S, buf, nrch, ncols):
        self.ap = buf.t
        self.ncell = (ncols + 127) // 128
        self.cells = [[S.dram(None, "%s_%d_%d" % (buf.name, r, c)) for c in range(self.ncell)] for r in range(nrch)]
        self.nrch = nrch

    def bufs(self, r0, r1, lo, hi):
        return [self.cells[r][c] for r in range(r0, r1) for c in range(lo // 128, (hi - 1) // 128 + 1)]


def emit_norm(S, C, xb, w, nk, gs, sh, col, ones, hout_fn, tmp_pool, dmodel, rstd_only=False):
    sq, rs, xn = tmp_pool["sq"], tmp_pool["rs"], tmp_pool["xn"]
    S.op("act", lambda e: e.activation(out=sq[:, :, :w], in_=xb[:, :, :w], func=AF.Square), reads=[xb], writes=[sq])
    p = S.psum()
    for k in range(nk):
        S.op("pe", lambda e, k=k: e.matmul(p[:, :w], lhsT=ones[:], rhs=sq[:, k, :w], start=(k == 0), stop=(k == nk - 1)),
             reads=[sq, ones], writes=[p], inc=(k == nk - 1))
    S.op("act", lambda e: e.activation(out=rs[:, :w], in_=p[:, :w], func=AF.Sqrt, scale=1.0 / dmodel, bias=tmp_pool["eps"][:]),
         reads=[p, tmp_pool["eps"]], writes=[rs])
    S.op("dve", lambda e: e.reciprocal(out=rs[:, :w], in_=rs[:, :w]), reads=[rs], writes=[rs])
    if rstd_only:
        return
    S.op("dve", lambda e: e.tensor_tensor(out=xn[:, :, :w], in0=xb[:, :, :w], in1=bcast_mid(rs[:, :w], nk), op=ALU.mult),
         reads=[xb, rs], writes=[xn])
    for k in range(nk):
        hb, hap = hout_fn(k)
        if sh is None:
            S.op("act", lambda e, k=k, hap=hap: e.activation(out=hap, in_=xn[:, k, :w], func=AF.Copy, scale=gs[:, k, col:col + 1]),
                 reads=[xn, gs], writes=[hb])
        else:
            S.op("act", lambda e, k=k, hap=hap: e.activation(out=hap, in_=xn[:, k, :w], func=AF.Identity,
                                                             scale=gs[:, k, col:col + 1], bias=sh[:, k, col:col + 1]),
                 reads=[xn, gs, sh], writes=[hb])


def emit_R(nc, S, C, T, merge, inproj, final):
    KD = C.KD
    NC = C.NC
    segs = [(0, C.NCTX, 1), (C.NCTX, NC, 0)]

    def segblocks(lo, hi, w=512):
        out = []
        for (a, b, col) in segs:
            a2, b2 = max(a, lo), min(b, hi)
            if a2 < b2:
                out += [(x0, x1, col) for (x0, x1) in blocks(a2, b2, w)]
        return out

    xin = T["xT"]
    xin_v = xin.t.rearrange("(k p) n -> p k n", p=128)
    xcur, xcur_v, xgrid = xin, xin_v, None

    with S.scope():
        ones = S.sb("ones", [128, 128], BF16)
        S.op("pool", lambda e: e.memset(ones[:], 1.0), writes=[ones])
        epsb = S.sb("epsb", [128, 1], F32)
        S.op("pool", lambda e: e.memset(epsb[:], C.EPS), writes=[epsb])
        cflag = S.sb("cflag", [128, 3], F32)
        S.load("sp", cflag, cflag[:], T["cflag"], T["cflag"].t)

        if inproj:
            modn = S.sb("modn", [128, 6 * KD, 2], F32)
            with S.scope():
                cT = S.sb("cT", [128, KD, 2], F32)
                S.load("sp", cT, cT[:], T["cT"], T["cT"].t)
                sc = S.sb("silu_c", [128, KD, 2], F32)
                S.op("act", lambda e: e.activation(out=sc[:], in_=cT[:], func=AF.Silu), reads=[cT], writes=[sc])
                bm = S.sb("bmod", [128, 6 * KD], F32)
                S.load("sp", bm, bm[:], T["bmod"], T["bmod"].t)
                MG = 4
                wms = [S.sb("wm%d" % i, [128, KD, MG * 128], F32) for i in range(2)]
                wv = T["wmod"].t.rearrange("(k p) m -> p k m", p=128)
                ng = (6 * KD) // MG
                S.load("sp", wms[0], wms[0][:], T["wmod"], wv[:, :, 0:MG * 128])
                for g in range(ng):
                    if g + 1 < ng:
                        S.load("sp", wms[(g + 1) % 2], wms[(g + 1) % 2][:], T["wmod"], wv[:, :, (g + 1) * MG * 128:(g + 2) * MG * 128])
                    wm = wms[g % 2]
                    p = S.psum()
                    for mi in range(MG):
                        for k in range(KD):
                            S.op("pe", lambda e, mi=mi, k=k, wm=wm, p=p: e.matmul(p[:, 2 * mi:2 * mi + 2], lhsT=wm[:, k, mi * 128:(mi + 1) * 128],
                                                                             rhs=sc[:, k, :], start=(k == 0), stop=(k == KD - 1)),
                                 reads=[wm, sc], writes=[p], inc=(k == KD - 1))
                    for mi in range(MG):
                        j = g * MG + mi
                        S.op("dve", lambda e, mi=mi, j=j, p=p: e.tensor_scalar(out=modn[:, j, :], in0=p[:, 2 * mi:2 * mi + 2], scalar1=bm[:, j:j + 1],
                                                                             scalar2=None, op0=ALU.add),
                             reads=[p, bm], writes=[modn])
                S.store("sp", T["modo"], T["modo"].t, modn, modn[:])

        if merge:
            xo = T["xTo"]
            xo_v = xo.t.rearrange("(k p) n -> p k n", p=128)
            xgrid = Grid(S, xo, KD, NC)
            h2s = T["h2s"]
            h2_v = h2s.t.rearrange("(k p) n -> p k n", p=128)
            BK = C.BW // 128
            modp = S.sb("modp", [128, 6 * KD, 2], F32)
            S.load("sp", modp, modp[:], T["modp"], T["modp"].t)
            nmp = S.sb("nmp", [128, KD], F32)
            S.load("sp", nmp, nmp[:], T["nmix_p"], T["nmix_p"].t)
            nfp = S.sb("nfp", [128, KD], F32)
            S.load("sp", nfp, nfp[:], T["nffn_p"], T["nffn_p"].t)
            gs1 = S.sb("gs1", [128, KD, 2], F32)
            gs2 = S.sb("gs2", [128, KD, 2], F32)
            for (gs, nm, goff) in ((gs1, nmp, 1), (gs2, nfp, 4)):
                for col in range(2):
                    S.op("dve", lambda e, gs=gs, nm=nm, goff=goff, col=col: e.scalar_tensor_tensor(
                        out=gs[:, :, col], in0=modp[:, goff * KD:(goff + 1) * KD, col], scalar=1.0, in1=nm[:], op0=ALU.add, op1=ALU.mult),
                        reads=[modp, nm], writes=[gs])
            sh1 = lambda k, col: modp[:, 0 * KD + k, col:col + 1]
            half = (NC + 1) // 2
            for (sb_lo, sb_hi) in ((0, half), (half, NC)):
                W = sb_hi - sb_lo
                with S.scope():
                    h1 = S.sb("h1", [128, KD, W], BF16)
                    y = S.sb("y", [128, KD, W], BF16)
                    with S.scope():
                        xbl = [S.sb("xbl%d" % i, [128, KD, 256], F32) for i in range(2)]
                        tp = {"sq": S.sb("sq", [128, KD, 256], BF16), "rs": S.sb("rs", [128, 256], F32),
                              "xn": S.sb("xn", [128, KD, 256], F32), "eps": epsb}
                        shb = S.sb("sh1b", [128, KD, 2], F32)
                        S.op("dve", lambda e: e.tensor_copy(out=shb[:], in_=modp[:, 0:KD, :]), reads=[modp], writes=[shb])
                        for bi, (lo, hi, col) in enumerate(segblocks(sb_lo, sb_hi, 256)):
                            xb = xbl[bi % 2]
                            S.load("sp", xb, xb[:, :, :hi - lo], xin, xin_v[:, :, lo:hi])
                            emit_norm(S, C, xb, hi - lo, KD, gs1, shb, col, ones,
                                      lambda k, lo=lo, hi=hi: (h1, h1[:, k, lo - sb_lo:hi - sb_lo]), tp, C.D)
                    with S.scope():
                        br = []
                        for i, n in enumerate(("AT", "BT", "CT")):
                            t = S.sb("br%d" % i, [128, BK, W], BF16)
                            S.load("sp", t, t[:], T[n], T[n].t.rearrange("(k p) n -> p k n", p=128)[:, :, sb_lo:sb_hi])
                            br.append(t)
                        wgs = [S.sb("wg%d" % i, [128, KD, 128], BF16) for i in range(3)]
                        wbs = [S.sb("wb%d" % i, [128, BK, 128], BF16) for i in range(3)]
                        sgs = [S.sb("sg%d" % i, [128, 512], F32) for i in range(3)]
                        tms = [S.sb("tm%d" % i, [128, 512], F32) for i in range(3)]
                        yac = [S.sb("yac%d" % i, [128, W], F32) for i in range(2)]
                        winv = T["win_p"].t.rearrange("(k p) m -> p k m", p=128)
                        it = 0

                        def ldw(m, i, it):
                            c0 = C.GOFF + i * C.D + m * 128
                            S.load("pool", wgs[it % 3], wgs[it % 3][:], T["win_p"], winv[:, :, c0:c0 + 128])
                            S.load("pool", wbs[it % 3], wbs[it % 3][:], T["wbr_p"],
                                   T["wbr_p"].t[i].rearrange("(k p) m -> p k m", p=128)[:, :, m * 128:(m + 1) * 128])
                        seq = [(m, i) for m in range(KD) for i in range(3)]
                        ldw(seq[0][0], seq[0][1], 0)
                        ldw(seq[1][0], seq[1][1], 1)
                        for it, (m, i) in enumerate(seq):
                            if it + 2 < len(seq):
                                ldw(seq[it + 2][0], seq[it + 2][1], it + 2)
                            wg, wb = wgs[it % 3], wbs[it % 3]
                            ya = yac[m % 2]
                            for bi, (lo, hi) in enumerate(blocks(0, W)):
                                w = hi - lo
                                pg = S.psum()
                                for k in range(KD):
                                    S.op("pe", lambda e, k=k, pg=pg, wg=wg, lo=lo, hi=hi, w=w: e.matmul(pg[:, :w], lhsT=wg[:, k, :], rhs=h1[:, k, lo:hi],
                                                                                                   start=(k == 0), stop=(k == KD - 1)),
                                         reads=[wg, h1], writes=[pg], inc=(k == KD - 1))
                                pb = S.psum()
                                for k in range(BK):
                                    S.op("pe", lambda e, k=k, pb=pb, wb=wb, lo=lo, hi=hi, w=w, i=i: e.matmul(pb[:, :w], lhsT=wb[:, k, :], rhs=br[i][:, k, lo:hi],
                                                                                                        start=(k == 0), stop=(k == BK - 1)),
                                         reads=[wb, br[i]], writes=[pb], inc=(k == BK - 1))
                                sg = sgs[(it * 3 + bi) % 3]
                                S.op("act", lambda e, sg=sg, pg=pg, w=w: e.activation(out=sg[:, :w], in_=pg[:, :w], func=AF.Sigmoid),
                                     reads=[pg], writes=[sg])
                                if i == 0:
                                    S.op("dve", lambda e, sg=sg, pb=pb, ya=ya, lo=lo, hi=hi, w=w: e.tensor_tensor(out=ya[:, lo:hi], in0=sg[:, :w], in1=pb[:, :w], op=ALU.mult),
                                         reads=[sg, pb], writes=[ya])
                                else:
                                    tm = tms[(it * 3 + bi) % 3]
                                    S.op("dve", lambda e, sg=sg, pb=pb, tm=tm, w=w: e.tensor_tensor(out=tm[:, :w], in0=sg[:, :w], in1=pb[:, :w], op=ALU.mult),
                                         reads=[sg, pb], writes=[tm])
                                    if i == 1:
                                        S.op("pool", lambda e, tm=tm, ya=ya, lo=lo, hi=hi, w=w: e.tensor_tensor(out=ya[:, lo:hi], in0=ya[:, lo:hi], in1=tm[:, :w], op=ALU.add),
                                             reads=[tm, ya], writes=[ya])
                                    else:
                                        S.op("pool", lambda e, tm=tm, ya=ya, lo=lo, hi=hi, w=w, m=m: e.tensor_tensor(out=y[:, m, lo:hi], in0=ya[:, lo:hi], in1=tm[:, :w], op=ALU.add),
                                             reads=[tm, ya], writes=[y])
                    with S.scope():
                        wos = [S.sb("wo%d" % i, [128, KD, 128], BF16) for i in range(2)]
                        xms = [S.sb("xm%d" % i, [128, 512], F32) for i in range(3)]
                        xds = [S.sb("xd%d" % i, [128, 512], F32) for i in range(3)]
                        sqs = [S.sb("sq%d" % i, [128, 512], BF16) for i in range(3)]
                        rs2 = S.sb("rs2", [128, W], F32)
                        sbl = segblocks(sb_lo, sb_hi)
                        ssq = [S.psum() for _ in sbl]
                        wov = T["wout_p"].t.rearrange("(k p) m -> p k m", p=128)
                        S.load("pool", wos[0], wos[0][:], T["wout_p"], wov[:, :, 0:128])
                        it = 0
                        for m in range(KD):
                            if m + 1 < KD:
                                S.load("pool", wos[(m + 1) % 2], wos[(m + 1) % 2][:], T["wout_p"], wov[:, :, (m + 1) * 128:(m + 2) * 128])
                            wo = wos[m % 2]
                            for bi, (lo, hi, col) in enumerate(sbl):
                                w = hi - lo
                                po = S.psum()
                                while po in ssq:
                                    po = S.psum()
                                for k in range(KD):
                                    S.op("pe", lambda e, k=k, po=po, wo=wo, lo=lo, hi=hi, w=w: e.matmul(po[:, :w], lhsT=wo[:, k, :], rhs=y[:, k, lo - sb_lo:hi - sb_lo],
                                                                                                   start=(k == 0), stop=(k == KD - 1)),
                                         reads=[wo, y], writes=[po], inc=(k == KD - 1))
                                xm, xd, sq = xms[it % 3], xds[it % 3], sqs[it % 3]
                                it += 1
                                S.load("sp", xm, xm[:, :w], xin, xin_v[:, m, lo:hi])
                                S.op("dve", lambda e, po=po, xm=xm, xd=xd, w=w, m=m, col=col: e.scalar_tensor_tensor(
                                    out=xd[:, :w], in0=po[:, :w], scalar=modp[:, 2 * KD + m, col:col + 1], in1=xm[:, :w], op0=ALU.mult, op1=ALU.add),
                                    reads=[po, xm, modp], writes=[xd])
                                S.op("act", lambda e, xd=xd, sq=sq, w=w: e.activation(out=sq[:, :w], in_=xd[:, :w], func=AF.Square), reads=[xd], writes=[sq])
                                S.op("pe", lambda e, bi=bi, sq=sq, w=w, m=m: e.matmul(ssq[bi][:, :w], lhsT=ones[:], rhs=sq[:, :w], start=(m == 0), stop=(m == KD - 1)),
                                     reads=[sq, ones], writes=[ssq[bi]], inc=True)
                                S.store("sp", xgrid.bufs(m, m + 1, lo, hi), xo_v[:, m, lo:hi], xd, xd[:, :w])
                        for bi, (lo, hi, col) in enumerate(sbl):
                            w = hi - lo
                            S.op("act", lambda e, bi=bi, lo=lo, hi=hi, w=w: e.activation(out=rs2[:, lo - sb_lo:hi - sb_lo], in_=ssq[bi][:, :w], func=AF.Sqrt,
                                                                                      scale=1.0 / C.D, bias=epsb[:]),
                                 reads=[ssq[bi], epsb], writes=[rs2])
                        S.op("dve", lambda e: e.reciprocal(out=rs2[:], in_=rs2[:]), reads=[rs2], writes=[rs2])
                        h2t = [S.sb("h2t%d" % i, [128, 512], BF16) for i in range(3)]
                        for m in range(KD):
                            for bi, (lo, hi, col) in enumerate(sbl):
                                w = hi - lo
                                xm, xd, ht = xms[it % 3], xds[it % 3], h2t[it % 3]
                                it += 1
                                S.load("sp", xm, xm[:, :w], xgrid.bufs(m, m + 1, lo, hi), xo_v[:, m, lo:hi])
                                S.op("dve", lambda e, xm=xm, xd=xd, lo=lo, hi=hi, w=w: e.tensor_tensor(out=xd[:, :w], in0=xm[:, :w], in1=rs2[:, lo - sb_lo:hi - sb_lo], op=ALU.mult),
                                     reads=[xm, rs2], writes=[xd])
                                S.op("act", lambda e, xd=xd, ht=ht, w=w, m=m, col=col: e.activation(out=ht[:, :w], in_=xd[:, :w], func=AF.Identity,
                                                                                             scale=gs2[:, m, col:col + 1], bias=modp[:, 3 * KD + m, col:col + 1]),
                                     reads=[xd, gs2, modp], writes=[ht])
                                S.store("sp", h2s, h2_v[:, m, lo:hi], ht, ht[:, :w])
            with S.scope():
                h2 = S.sb("h2", [128, KD, NC], BF16)
                for k in range(KD):
                    S.load("sp", h2, h2[:, k, :], h2s, h2_v[:, k, :], part=True)
                for j, cidx in enumerate((C.NCTX, C.NCTX + 1, NC - 1)):
                    S.op("dve", lambda e, j=j, cidx=cidx: e.tensor_scalar(out=h2[:, :, cidx:cidx + 1], in0=h2[:, :, cidx:cidx + 1], scalar1=cflag[:, j:j + 1],
                                                                        scalar2=None, op0=ALU.mult),
                         reads=[h2, cflag], writes=[h2])
                cw = S.sb("cw", [128, 3, 2 * C.FCH], F32)
                S.load("sp", cw, cw[:], T["cw_p"], T["cw_p"].t)
                cb = S.sb("cb", [128, 2 * C.FCH], F32)
                S.load("sp", cb, cb[:], T["cb_p"], T["cb_p"].t)
                JG = C.JG
                act = S.sb("act", [128, JG, NC], BF16)
                zr = [S.sb("zr%d" % i, [128, NC], F32) for i in range(2)]
                ut = [S.sb("ut%d" % i, [128, NC], F32) for i in range(2)]
                wups = [S.sb("wup%d" % i, [128, KD, 256], BF16) for i in range(2)]
                wdns = [S.sb("wdn%d" % i, [128, JG, 128], BF16) for i in range(2)]
                xms = [S.sb("fxm%d" % i, [128, 512], F32) for i in range(3)]
                xds = [S.sb("fxd%d" % i, [128, 512], F32) for i in range(3)]
                wupv = T["wup_p"].t.rearrange("(k p) m -> p k m", p=128)
                wdnv = T["wdn_p"].t.rearrange("(j p) m -> p j m", p=128)
                allblk = blocks(0, NC)
                sbl = segblocks(0, NC)

                def ldup(j):
                    t = wups[j % 2]
                    S.load("pool", t, t[:, :, 0:128], T["wup_p"], wupv[:, :, j * 128:(j + 1) * 128])
                    S.load("pool", t, t[:, :, 128:256], T["wup_p"], wupv[:, :, C.DFF + j * 128:C.DFF + (j + 1) * 128], part=True)
                ldup(0)
                it = 0
                ngroups = (C.FCH + JG - 1) // JG
                for g in range(ngroups):
                    j0, j1 = g * JG, min((g + 1) * JG, C.FCH)
                    for j in range(j0, j1):
                        if j + 1 < C.FCH:
                            ldup(j + 1)
                        wu = wups[j % 2]
                        for half_i, ch in enumerate((j, C.FCH + j)):
                            z, u = zr[half_i], ut[half_i]
                            for (lo, hi) in allblk:
                                w = hi - lo
                                p = S.psum()
                                for k in range(KD):
                                    S.op("pe", lambda e, k=k, p=p, wu=wu, half_i=half_i, lo=lo, hi=hi, w=w: e.matmul(
                                        p[:, :w], lhsT=wu[:, k, half_i * 128:(half_i + 1) * 128], rhs=h2[:, k, lo:hi], start=(k == 0), stop=(k == KD - 1)),
                                        reads=[wu, h2], writes=[p], inc=(k == KD - 1))
                                S.op("act", lambda e, p=p, z=z, lo=lo, hi=hi, w=w: e.copy(out=z[:, lo:hi], in_=p[:, :w]), reads=[p], writes=[z])
                            S.op("act", lambda e, z=z, u=u, ch=ch: e.activation(out=u[:], in_=z[:], func=AF.Identity, scale=cw[:, 1, ch:ch + 1], bias=cb[:, ch:ch + 1]),
                                 reads=[z, cw, cb], writes=[u])
                            S.op("dve", lambda e, z=z, u=u, ch=ch: e.scalar_tensor_tensor(out=u[:, 1:NC], in0=z[:, 0:NC - 1], scalar=cw[:, 0, ch:ch + 1], in1=u[:, 1:NC],
                                                                                       op0=ALU.mult, op1=ALU.add), reads=[z, u, cw], writes=[u])
                            S.op("dve", lambda e, z=z, u=u, ch=ch: e.scalar_tensor_tensor(out=u[:, 0:NC - 1], in0=z[:, 1:NC], scalar=cw[:, 2, ch:ch + 1], in1=u[:, 0:NC - 1],
                                                                                       op0=ALU.mult, op1=ALU.add), reads=[z, u, cw], writes=[u])
                        S.op("act", lambda e: e.activation(out=ut[1][:], in_=ut[1][:], func=AF.Silu), reads=[ut[1]], writes=[ut[1]])
                        S.op("pool", lambda e, j=j, j0=j0: e.tensor_tensor(out=act[:, j - j0, :], in0=ut[1][:], in1=ut[0][:], op=ALU.mult),
                             reads=[ut[0], ut[1]], writes=[act])
                    nj = j1 - j0
                    wd0 = wdns[0]
                    S.load("pool", wd0, wd0[:, :nj, :], T["wdn_p"], wdnv[:, j0:j1, 0:128])
                    for m in range(KD):
                        if m + 1 < KD:
                            t = wdns[(m + 1) % 2]
                            S.load("pool", t, t[:, :nj, :], T["wdn_p"], wdnv[:, j0:j1, (m + 1) * 128:(m + 2) * 128])
                        wd = wdns[m % 2]
                        for (lo, hi, col) in sbl:
                            w = hi - lo
                            p = S.psum()
                            for jj in range(nj):
                                S.op("pe", lambda e, jj=jj, p=p, wd=wd, lo=lo, hi=hi, w=w: e.matmul(p[:, :w], lhsT=wd[:, jj, :], rhs=act[:, jj, lo:hi],
                                                                                               start=(jj == 0), stop=(jj == nj - 1)),
                                     reads=[wd, act], writes=[p], inc=(jj == nj - 1))
                            xm, xd = xms[it % 3], xds[it % 3]
                            it += 1
                            cells = xgrid.bufs(m, m + 1, lo, hi)
                            S.load("sp", xm, xm[:, :w], cells, xo_v[:, m, lo:hi])
                            S.op("dve", lambda e, p=p, xm=xm, xd=xd, w=w, m=m, col=col: e.scalar_tensor_tensor(
                                out=xd[:, :w], in0=p[:, :w], scalar=modp[:, 5 * KD + m, col:col + 1], in1=xm[:, :w], op0=ALU.mult, op1=ALU.add),
                                reads=[p, xm, modp], writes=[xd])
                            S.store("sp", cells, xo_v[:, m, lo:hi], xd, xd[:, :w])
            xcur, xcur_v = xo, xo_v

        def xbufs(lo, hi):
            return xgrid.bufs(0, KD, lo, hi) if xgrid is not None else [xcur]

        if final:
            with S.scope():
                nf = S.sb("nfin", [128, KD], F32)
                S.load("sp", nf, nf[:], T["nfin"], T["nfin"].t)
                nf3 = S.sb("nf3", [128, KD, 1], F32)
                S.op("dve", lambda e: e.tensor_copy(out=nf3[:, :, 0], in_=nf[:]), reads=[nf], writes=[nf3])
                xbl = [S.sb("xbl%d" % i, [128, KD, 256], F32) for i in range(2)]
                obl = [S.sb("obl%d" % i, [128, KD, 256], F32) for i in range(2)]
                tp = {"sq": S.sb("sq", [128, KD, 256], BF16), "rs": S.sb("rs", [128, 256], F32),
                      "xn": S.sb("xn", [128, KD, 256], F32), "eps": epsb}
                ov = T["oT"].t.rearrange("(k p) n -> p k n", p=128)
                for bi, (lo, hi) in enumerate(blocks(C.OWN0, C.OWN0 + C.NOWN, 256)):
                    xb, ob = xbl[bi % 2], obl[bi % 2]
                    S.load("sp", xb, xb[:, :, :hi - lo], xbufs(lo, hi), xcur_v[:, :, lo:hi])
                    emit_norm(S, C, xb, hi - lo, KD, nf3, None, 0, ones, lambda k, ob=ob, lo=lo, hi=hi: (ob, ob[:, k, :hi - lo]), tp, C.D)
                    S.store("sp", T["oT"], ov[:, :, lo - C.OWN0:hi - C.OWN0], ob, ob[:, :, :hi - lo])

        if inproj:
            emit_inproj(nc, S, C, T, modn, xcur_v, xbufs, ones, epsb)


def emit_inproj(nc, S, C, T, modn, xcur_v, xbufs, ones, epsb):
    KD, NT = C.KD, C.NT
    with S.scope():
        hT = S.sb("hT", [128, KD, NT], BF16)
        with S.scope():
            nm = S.sb("nm", [128, KD], F32)
            S.load("sp", nm, nm[:], T["nmix"], T["nmix"].t)
            gs = S.sb("gs", [128, KD, 2], F32)
            shb = S.sb("shb", [128, KD, 2], F32)
            for col in range(2):
                S.op("dve", lambda e, col=col: e.scalar_tensor_tensor(out=gs[:, :, col], in0=modn[:, KD:2 * KD, col], scalar=1.0, in1=nm[:],
                                                                    op0=ALU.add, op1=ALU.mult), reads=[modn, nm], writes=[gs])
            S.op("dve", lambda e: e.tensor_copy(out=shb[:], in_=modn[:, 0:KD, :]), reads=[modn], writes=[shb])
            xbl = [S.sb("xbl%d" % i, [128, KD, 256], F32) for i in range(2)]
            tp = {"sq": S.sb("sq", [128, KD, 256], BF16), "rs": S.sb("rs", [128, 256], F32),
                  "xn": S.sb("xn", [128, KD, 256], F32), "eps": epsb}
            bl = [(lo, hi, lo, 1) for (lo, hi) in blocks(0, C.NCTX, 256)] + \
                 [(lo, hi, lo - C.OWN0 + C.NCTX, 0) for (lo, hi) in blocks(C.OWN0, C.OWN0 + C.NOWN, 256)]
            for bi, (lo, hi, ho, col) in enumerate(bl):
                xb = xbl[bi % 2]
                S.load("sp", xb, xb[:, :, :hi - lo], xbufs(lo, hi), xcur_v[:, :, lo:hi])
                emit_norm(S, C, xb, hi - lo, KD, gs, shb, col, ones, lambda k, ho=ho, lo=lo, hi=hi: (hT, hT[:, k, ho:ho + hi - lo]), tp, C.D)
        with S.scope():
            winv = T["win"].t.rearrange("(k p) m -> p k m", p=128)
            wts = [S.sb("wt%d" % i, [128, KD, 512], BF16) for i in range(2)]
            o16 = [S.sb("o16_%d" % i, [128, 512], BF16) for i in range(4)]
            o32 = [S.sb("o32_%d" % i, [128, 512], F32) for i in range(3)]
            g32 = [S.sb("g32_%d" % i, [128, 512], F32) for i in range(3)]
            cblk = blocks(0, C.NCTX) + blocks(C.NCTX, NT)
            offs = {}
            o = 0
            for n, wdt in (("Q", C.NAW), ("K", C.NAW), ("V", C.NAW), ("LX", C.LRUW), ("LG", C.LRUW), ("CQ", C.QR), ("CKV", C.KVR + C.ROPE)):
                offs[n] = o
                o += wdt
            groups = []
            for n, wdt in (("Q", C.NAW), ("K", C.NAW), ("V", C.NAW), ("LX", C.LRUW), ("LG", C.LRUW), ("CQ", C.QR), ("CKV", C.KVR + C.ROPE)):
                for c0 in range(0, wdt, 512):
                    groups.append((n, c0, min(512, wdt - c0)))
            st = {"it": 0}

            def ldw(gi):
                n, c0, gw = groups[gi]
                t = wts[gi % 2]
                S.load("pool", t, t[:, :, :gw], T["win"], winv[:, :, offs[n] + c0:offs[n] + c0 + gw])

            def mm(p, wt, m0, mw, lo, hi):
                w = hi - lo
                for k in range(KD):
                    S.op("pe", lambda e, k=k: e.matmul(p[:mw, :w], lhsT=wt[:, k, m0:m0 + mw], rhs=hT[:, k, lo:hi], start=(k == 0), stop=(k == KD - 1)),
                         reads=[wt, hT], writes=[p], inc=(k == KD - 1))

            nrm = {"sq": S.sb("nsq", [128, 4, 512], BF16), "rs": S.sb("nrs", [128, 512], F32), "xn": S.sb("nxn", [128, 4, 512], F32), "eps": epsb}
            cqb = [S.sb("cqb%d" % i, [128, 4, 512], F32) for i in range(2)]
            qn = S.sb("qn", [128, C.QR // 128, 1], F32)
            S.load("sp", qn, qn[:, :, 0], T["qn"], T["qn"].t)
            kvn = S.sb("kvn", [128, C.KVR // 128, 1], F32)
            S.load("sp", kvn, kvn[:, :, 0], T["kvn"], T["kvn"].t)
            cos2 = S.sb("cos2", [C.ROPE, C.NOWN], F32)
            S.load("sp", cos2, cos2[:], T["cos2"], T["cos2"].t)
            sin2 = S.sb("sin2", [C.ROPE, C.NOWN], F32)
            S.load("sp", sin2, sin2[:], T["sin2"], T["sin2"].t)
            rotm = S.sb("rotm", [C.ROPE, C.ROPE], F32)
            S.load("sp", rotm, rotm[:], T["rotm"], T["rotm"].t)
            ldw(0)
            for gi, (n, c0, gw) in enumerate(groups):
                if gi + 1 < len(groups):
                    ldw(gi + 1)
                wt = wts[gi % 2]
                if n in ("Q", "K", "LX", "LG"):
                    dst = {"Q": "QT", "K": "KT", "LX": "LXT", "LG": "GGT"}[n]
                    dv = T[dst].t
                    for m0 in range(0, gw, 128):
                        r0 = c0 + m0
                        for (lo, hi) in cblk:
                            w = hi - lo
                            p = S.psum()
                            mm(p, wt, m0, 128, lo, hi)
                            it = st["it"]
                            st["it"] += 1
                            if n == "LX":
                                ot = o32[it % 3]
                                S.op("act" if it % 2 else "dve", (lambda e, p=p, ot=ot, w=w: e.copy(out=ot[:, :w], in_=p[:, :w])) if it % 2 else
                                     (lambda e, p=p, ot=ot, w=w: e.tensor_copy(out=ot[:, :w], in_=p[:, :w])), reads=[p], writes=[ot])
                            elif n == "LG":
                                ot, g = o16[it % 4], g32[it % 3]
                                S.op("act", lambda e, p=p, g=g, w=w: e.activation(out=g[:, :w], in_=p[:, :w], func=AF.Square), reads=[p], writes=[g])
                                S.op("dve", lambda e, g=g, w=w: e.tensor_scalar(out=g[:, :w], in0=g[:, :w], scalar1=0.044715, scalar2=1.0, op0=ALU.mult, op1=ALU.add),
                                     reads=[g], writes=[g])
                                S.op("dve", lambda e, g=g, p=p, w=w: e.tensor_tensor(out=g[:, :w], in0=g[:, :w], in1=p[:, :w], op=ALU.mult), reads=[g, p], writes=[g])
                                S.op("act", lambda e, g=g, w=w: e.activation(out=g[:, :w], in_=g[:, :w], func=AF.Sigmoid, scale=1.5957691216057308),
                                     reads=[g], writes=[g])
                                S.op("dve", lambda e, g=g, p=p, ot=ot, w=w: e.tensor_tensor(out=ot[:, :w], in0=g[:, :w], in1=p[:, :w], op=ALU.mult),
                                     reads=[g, p], writes=[ot])
                            else:
                                ot = o16[it % 4]
                                S.op("act" if it % 2 else "dve", (lambda e, p=p, ot=ot, w=w: e.copy(out=ot[:, :w], in_=p[:, :w])) if it % 2 else
                                     (lambda e, p=p, ot=ot, w=w: e.tensor_copy(out=ot[:, :w], in_=p[:, :w])), reads=[p], writes=[ot])
                            S.store("sp", T[dst], dv[r0:r0 + 128, lo:hi], ot, ot[:, :w])
                elif n == "V":
                    dv = T["V"].t
                    for tt in range(NT // 128):
                        p = S.psum()
                        for k in range(KD):
                            S.op("pe", lambda e, k=k, p=p, tt=tt: e.matmul(p[:, :gw], lhsT=hT[:, k, tt * 128:(tt + 1) * 128], rhs=wt[:, k, :gw],
                                                                        start=(k == 0), stop=(k == KD - 1)),
                                 reads=[wt, hT], writes=[p], inc=(k == KD - 1))
                        it = st["it"]
                        st["it"] += 1
                        ot = o16[it % 4]
                        S.op("act" if it % 2 else "dve", (lambda e, p=p, ot=ot: e.copy(out=ot[:, :gw], in_=p[:, :gw])) if it % 2 else
                             (lambda e, p=p, ot=ot: e.tensor_copy(out=ot[:, :gw], in_=p[:, :gw])), reads=[p], writes=[ot])
                        S.store("sp", T["V"], dv[tt * 128:(tt + 1) * 128, c0:c0 + gw], ot, ot[:, :gw])
                elif n == "CQ":
                    nq = C.QR // 128
                    for bi, (lo, hi) in enumerate(cblk):
                        w = hi - lo
                        cq = cqb[bi % 2]
                        for m in range(nq):
                            p = S.psum()
                            mm(p, wt, m * 128, 128, lo, hi)
                            S.op("act" if m % 2 else "dve", (lambda e, p=p, cq=cq, m=m, w=w: e.copy(out=cq[:, m, :w], in_=p[:, :w])) if m % 2 else
                                 (lambda e, p=p, cq=cq, m=m, w=w: e.tensor_copy(out=cq[:, m, :w], in_=p[:, :w])), reads=[p], writes=[cq])
                        outs = []

                        def hout(k, lo=lo, hi=hi, w=w, outs=outs):
                            ot = o16[st["it"] % 4]
                            st["it"] += 1
                            outs.append((k, ot))
                            return ot, ot[:, :w]
                        emit_norm_store(S, C, cq, w, nq, qn, ones, nrm, C.QR, o16, st, T["CQN"], lo, hi)
                else:
                    nkv = C.KVR // 128
                    for bi, (lo, hi) in enumerate(cblk):
                        w = hi - lo
                        cq = cqb[bi % 2]
                        for m in range(nkv):
                            p = S.psum()
                            mm(p, wt, m * 128, 128, lo, hi)
                            S.op("act" if m % 2 else "dve", (lambda e, p=p, cq=cq, m=m, w=w: e.copy(out=cq[:, m, :w], in_=p[:, :w])) if m % 2 else
                                 (lambda e, p=p, cq=cq, m=m, w=w: e.tensor_copy(out=cq[:, m, :w], in_=p[:, :w])), reads=[p], writes=[cq])
                        emit_norm_store(S, C, cq, w, nkv, kvn, ones, nrm, C.KVR, o16, st, T["CKVN"], lo, hi)
                        R = C.ROPE
                        p = S.psum()
                        mm(p, wt, C.KVR, R, lo, hi)
                        ot = o16[st["it"] % 4]
                        st["it"] += 1
                        if lo < C.NCTX:
                            S.op("act", lambda e, p=p, ot=ot, w=w: e.copy(out=ot[:R, :w], in_=p[:R, :w]), reads=[p], writes=[ot])
                        else:
                            raw, t1 = o32[st["it"] % 3], g32[st["it"] % 3]
                            pc = slice(lo - C.NCTX, hi - C.NCTX)
                            S.op("act", lambda e, p=p, raw=raw, w=w: e.copy(out=raw[:R, :w], in_=p[:R, :w]), reads=[p], writes=[raw])
                            p2 = S.psum()
                            S.op("pe", lambda e, p2=p2, raw=raw, w=w: e.matmul(p2[:R, :w], lhsT=rotm[:], rhs=raw[:R, :w], start=True, stop=True),
                                 reads=[rotm, raw], writes=[p2])
                            S.op("dve", lambda e, raw=raw, t1=t1, w=w, pc=pc: e.tensor_tensor(out=t1[:R, :w], in0=raw[:R, :w], in1=cos2[:, pc], op=ALU.mult),
                                 reads=[raw, cos2], writes=[t1])
                            S.op("dve", lambda e, raw=raw, p2=p2, w=w, pc=pc: e.tensor_tensor(out=raw[:R, :w], in0=p2[:R, :w], in1=sin2[:, pc], op=ALU.mult),
                                 reads=[p2, sin2], writes=[raw])
                            S.op("dve", lambda e, raw=raw, t1=t1, ot=ot, w=w: e.tensor_tensor(out=ot[:R, :w], in0=raw[:R, :w], in1=t1[:R, :w], op=ALU.add),
                                 reads=[raw, t1], writes=[ot])
                        S.store("sp", T["KRT"], T["KRT"].t[:, lo:hi], ot, ot[:R, :w])


def emit_norm_store(S, C, xb, w, nk, gn, ones, nrm, dmodel, o16, st, dst, lo, hi):
    outs = []

    def hout(k):
        ot = o16[st["it"] % 4]
        st["it"] += 1
        outs.append((k, ot))
        return ot, ot[:, :w]
    sq, rs, xn = nrm["sq"], nrm["rs"], nrm["xn"]
    S.op("act", lambda e: e.activation(out=sq[:, :nk, :w], in_=xb[:, :nk, :w], func=AF.Square), reads=[xb], writes=[sq])
    p = S.psum()
    for k in range(nk):
        S.op("pe", lambda e, k=k: e.matmul(p[:, :w], lhsT=ones[:], rhs=sq[:, k, :w], start=(k == 0), stop=(k == nk - 1)),
             reads=[sq, ones], writes=[p], inc=(k == nk - 1))
    S.op("act", lambda e: e.activation(out=rs[:, :w], in_=p[:, :w], func=AF.Sqrt, scale=1.0 / dmodel, bias=nrm["eps"][:]),
         reads=[p, nrm["eps"]], writes=[rs])
    S.op("dve", lambda e: e.reciprocal(out=rs[:, :w], in_=rs[:, :w]), reads=[rs], writes=[rs])
    for k in range(nk):
        ot, oap = hout(k)
        S.op("dve", lambda e, k=k, oap=oap: e.scalar_tensor_tensor(out=oap, in0=xb[:, k, :w], scalar=gn[:, k, 0:1], in1=rs[:, :w], op0=ALU.mult, op1=ALU.mult),
             reads=[xb, gn, rs], writes=[ot])
        S.store("sp", dst, dst.t[k * 128:(k + 1) * 128, lo:hi], ot, ot[:, :w])


def pp(v):
    return np.ascontiguousarray(np.asarray(v, np.float32).reshape(-1, 128).T)


def rope_tables(C):
    t = np.arange(C.SEQ)
    row = (t // C.GW).astype(np.float32)
    col = (t % C.GW).astype(np.float32)
    nf = C.ROPE // 4
    inv = (np.float32(C.THETA) ** (-np.arange(nf, dtype=np.float32) / np.float32(nf))).astype(np.float32)
    ang = np.concatenate([row[:, None] * inv, col[:, None] * inv], -1).astype(np.float32)
    cos, sin = np.cos(ang).astype(np.float32), np.sin(ang).astype(np.float32)
    cos2 = np.ascontiguousarray(np.concatenate([cos, cos], -1).T)
    sin2 = np.ascontiguousarray(np.concatenate([sin, sin], -1).T)
    h = C.ROPE // 2
    rotm = np.zeros((C.ROPE, C.ROPE), np.float32)
    for i in range(h):
        rotm[i + h, i] = -1.0
        rotm[i, i + h] = 1.0
    return cos2, sin2, rotm


def build_prog(C, kind, **flags):
    nc = bass.Bass("TRN2", target_bir_lowering=False)
    es = ExitStack()
    S = Sched(nc, es)
    S.init_psum(es)
    if kind == "R":
        ins, outs = r_io(C, **flags)
        T = declare(nc, S, ins, outs)
        emit_R(nc, S, C, T, **flags)
    else:
        ins, outs = m_io(C)
        T = declare(nc, S, ins, outs)
        emit_M(nc, S, C, T)
    deps = {}
    for n in outs:
        S._merge(deps, T[n].wr)
    S.barrier()
    S._wait("sp", deps)
    es.close()
    return nc, ins, outs, S


def m_io(C):
    HS, LS, MS = C.NAH // 2, C.LRUB // 2, C.MH // 2
    NTM = C.NTM
    ins = {
        "ident": ([128, 128], F32),
        "QT": ([HS * 128, NTM], BF16), "KT": ([HS * 128, NTM], BF16), "V": ([NTM, HS * 128], BF16),
        "BTin": ([128, HS, 14, 64], F32), "mask": ([128, 64], F32),
        "LXT": ([LS * 128, NTM], F32), "GGT": ([LS * 128, NTM], BF16),
        "lcw": ([128, 4, LS], F32), "lcb": ([128, LS], F32),
        "lwa": ([2, LS, 128, 128], F32), "lwx": ([2, LS, 128, 128], F32),
        "lba": ([128, 2, LS], F32), "lbx": ([128, 2, LS], F32), "llam": ([128, 2, LS], F32),
        "CQN": ([C.QR, NTM], BF16), "CKVN": ([C.KVR, NTM], BF16), "KRT": ([C.ROPE, NTM], BF16),
        "wq": ([C.QR, MS * C.QKD], F32), "wkv": ([C.KVR, MS * (C.NOPE + C.VD)], F32),
        "cos2": ([C.ROPE, C.SEQ], F32), "sin2": ([C.ROPE, C.SEQ], F32), "rotm": ([C.ROPE, C.ROPE], F32),
    }
    outs = {"AT": ([HS * 128, NTM], BF16), "BoT": ([LS * 128, NTM], BF16), "CoT": ([MS * C.VD, NTM], BF16)}
    return ins, outs


def emit_M(nc, S, C, T):
    HS, LS, MS = C.NAH // 2, C.LRUB // 2, C.MH // 2
    NTM, NCTX = C.NTM, C.NCTX
    NCC = NCTX // 128
    with S.scope():
        ones = S.sb("ones", [128, 128], BF16)
        S.op("pool", lambda e: e.memset(ones[:], 1.0), writes=[ones])
        id32 = S.sb("id32", [128, 128], F32)
        S.load("sp", id32, id32[:], T["ident"], T["ident"].t)
        ident = S.sb("ident", [128, 128], BF16)
        S.op("dve", lambda e: e.tensor_copy(out=ident[:], in_=id32[:]), reads=[id32], writes=[ident])

        with S.scope():
            btin = S.sb("btin", [128, HS * 14, 64], F32)
            S.load("sp", btin, btin[:], T["BTin"], T["BTin"].t.rearrange("p h r q -> p (h r) q"))
            msk = S.sb("msk", [128, 64], F32)
            S.load("sp", msk, msk[:], T["mask"], T["mask"].t)
            btb = S.sb("btb", [128, HS * 14, 64], BF16)
            S.op("dve", lambda e: e.scalar_tensor_tensor(out=btb[:], in0=btin[:], scalar=float(C.HD) ** 0.5, in1=bcast_mid(msk[:], HS * 14),
                                                         op0=ALU.mult, op1=ALU.add), reads=[btin, msk], writes=[btb])
            scale = float(C.HD) ** -0.5
            NTILE = NTM // 128
            NODD = (C.ROWS - 2) // 2
            bufs = []
            for i in range(2):
                bufs.append(dict(q=S.sb("naq%d" % i, [128, NTM], BF16), k=S.sb("nak%d" % i, [128, NTM], BF16),
                                 ve=S.sb("nave%d" % i, [128, NTILE, 128], BF16), vo=S.sb("navo%d" % i, [128, NODD, 128], BF16),
                                 o=S.sb("nao%d" % i, [128, NTM], BF16)))
            pts = [S.sb("napt%d" % i, [128, 512], BF16) for i in range(4)]
            rls = [S.sb("narl%d" % i, [128, 256], F32) for i in range(3)]

            def ldhead(h):
                B = bufs[h % 2]
                S.load("sp", B["q"], B["q"][:], T["QT"], T["QT"].t[h * 128:(h + 1) * 128, :])
                S.load("sp", B["k"], B["k"][:], T["KT"], T["KT"].t[h * 128:(h + 1) * 128, :])
                S.load("sp", B["ve"], B["ve"][:], T["V"], T["V"].t.rearrange("(i p) c -> p i c", p=128)[:, :, h * 128:(h + 1) * 128])
                S.load("sp", B["vo"], B["vo"][:], T["V"],
                       T["V"].t[NCTX + 64:NCTX + 64 + 128 * NODD, h * 128:(h + 1) * 128].rearrange("(i p) c -> p i c", p=128))
            ldhead(0)
            st = {"it": 0}
            for h in range(HS):
                if h + 1 < HS:
                    ldhead(h + 1)
                B = bufs[h % 2]
                items = []
                kch = [(jc * 128, None, ("ve", jc)) for jc in range(NCC)]
                items.append((0, NCTX, list(kch)))
                for r in range(C.ROWS):
                    kr0 = min(max(r - C.WINH // 2, 0), C.ROWS - C.WINH)
                    ch = []
                    for j in range(C.WINH // 2):
                        kr = kr0 + 2 * j
                        ri = kr - r + (C.WINH - 1)
                        vt = ("ve", (NCTX + 64 * kr) // 128) if kr % 2 == 0 else ("vo", (kr - 1) // 2)
                        ch.append((NCTX + 64 * kr, h * 14 + ri, vt))
                    items.append((NCTX + 64 * r, 64, ch + kch))

                def scores(item):
                    qlo, qw, ch = item
                    ps = S.psum()
                    for j, (kc, bi, vt) in enumerate(ch):
                        S.op("pe", lambda e, j=j, kc=kc: e.matmul(ps[:, j * qw:(j + 1) * qw], lhsT=B["k"][:, kc:kc + 128], rhs=B["q"][:, qlo:qlo + qw],
                                                                 start=True, stop=(bi is None)),
                             reads=[B["k"], B["q"]], writes=[ps], inc=False)
                        if bi is not None:
                            S.op("pe", lambda e, j=j, bi=bi: e.matmul(ps[:, j * qw:(j + 1) * qw], lhsT=ident[:], rhs=btb[:, bi, :], start=False, stop=True),
                                 reads=[ident, btb], writes=[ps], inc=False)
                    pt = pts[st["it"] % 4]
                    st["it"] += 1
                    n = len(ch) * qw
                    S.cnt["pe"] += 1
                    S.engs["pe"].nop().then_inc(S.sem["pe"], 1) if False else None
                    return ps, pt, n

                pending = None
                seq = items
                def finish_scores(ps, pt, n):
                    S.op("act", lambda e: e.activation(out=pt[:, :n], in_=ps[:, :n], func=AF.Exp, scale=scale), reads=[ps], writes=[pt])

                def pv(item, pt):
                    qlo, qw, ch = item
                    po = S.psum()
                    nchk = len(ch)
                    for j, (kc, bi, vt) in enumerate(ch):
                        vap = B[vt[0]][:, vt[1], :]
                        S.op("pe", lambda e, j=j, vap=vap: e.matmul(po[:, 0:qw], lhsT=vap, rhs=pt[:, j * qw:(j + 1) * qw], start=(j == 0), stop=(j == nchk - 1)),
                             reads=[B[vt[0]], pt], writes=[po], inc=False)
                    for j in range(nchk):
                        S.op("pe", lambda e, j=j: e.matmul(po[:, qw:2 * qw], lhsT=ones[:], rhs=pt[:, j * qw:(j + 1) * qw], start=(j == 0), stop=(j == nchk - 1)),
                             reads=[ones, pt], writes=[po], inc=(j == nchk - 1))
                    rl = rls[st["it"] % 3]
                    S.op("dve", lambda e: e.reciprocal(out=rl[:, :qw], in_=po[:, qw:2 * qw]), reads=[po], writes=[rl])
                    S.op("dve", lambda e: e.tensor_tensor(out=B["o"][:, qlo:qlo + qw], in0=po[:, 0:qw], in1=rl[:, :qw], op=ALU.mult),
                         reads=[po, rl], writes=[B["o"]])

                prev = None
                for item in seq:
                    qlo, qw, ch = item
                    ps = S.psum()
                    nchk = len(ch)
                    for j, (kc, bi, vt) in enumerate(ch):
                        last = (j == nchk - 1)
                        S.op("pe", lambda e, j=j, kc=kc, bi=bi: e.matmul(ps[:, j * qw:(j + 1) * qw], lhsT=B["k"][:, kc:kc + 128], rhs=B["q"][:, qlo:qlo + qw],
                                                                        start=True, stop=(bi is None)),
                             reads=[B["k"], B["q"]], writes=[ps], inc=(last and bi is None))
                        if bi is not None:
                            S.op("pe", lambda e, j=j, bi=bi: e.matmul(ps[:, j * qw:(j + 1) * qw], lhsT=ident[:], rhs=btb[:, bi, :], start=False, stop=True),
                                 reads=[ident, btb], writes=[ps], inc=last)
                    pt = pts[st["it"] % 4]
                    st["it"] += 1
                    n = nchk * qw
                    S.op("act", lambda e, ps=ps, pt=pt, n=n: e.activation(out=pt[:, :n], in_=ps[:, :n], func=AF.Exp, scale=scale), reads=[ps], writes=[pt])
                    if prev is not None:
                        pv(*prev)
                    prev = (item, pt)
                pv(*prev)
                S.store("sp", T["AT"], T["AT"].t[h * 128:(h + 1) * 128, :], B["o"], B["o"][:])

        with S.scope():
            lcw = S.sb("lcw", [128, 4, LS], F32)
            S.load("sp", lcw, lcw[:], T["lcw"], T["lcw"].t)
            lcb = S.sb("lcb", [128, LS], F32)
            S.load("sp", lcb, lcb[:], T["lcb"], T["lcb"].t)
            lba = S.sb("lba", [128, 2, LS], F32)
            S.load("sp", lba, lba[:], T["lba"], T["lba"].t)
            lbx = S.sb("lbx", [128, 2, LS], F32)
            S.load("sp", lbx, lbx[:], T["lbx"], T["lbx"].t)
            lam = S.sb("lam", [128, 2, LS], F32)
            S.load("sp", lam, lam[:], T["llam"], T["llam"].t)
            tiny = S.sb("tiny", [128, 1], F32)
            S.op("pool", lambda e: e.memset(tiny[:], 1e-20), writes=[tiny])
            ep = S.sb("ep", [128, 2, LS], F32)
            sp_ = S.sb("sp_", [128, 2, LS], F32)
            S.op("act", lambda e: e.activation(out=ep[:], in_=lam[:], func=AF.Exp, scale=-1.0), reads=[lam], writes=[ep])
            S.op("dve", lambda e: e.tensor_scalar(out=sp_[:], in0=ep[:], scalar1=-1.0 / 5.0, scalar2=1.0 / 4.0, op0=ALU.mult, op1=ALU.add), reads=[ep], writes=[sp_])
            for cst in (1.0 / 3.0, 1.0 / 2.0, 1.0):
                S.op("dve", lambda e: e.tensor_tensor(out=sp_[:], in0=sp_[:], in1=ep[:], op=ALU.mult), reads=[sp_, ep], writes=[sp_])
                S.op("dve", lambda e, cst=cst: e.tensor_scalar(out=sp_[:], in0=sp_[:], scalar1=-1.0, scalar2=cst, op0=ALU.mult, op1=ALU.add), reads=[sp_], writes=[sp_])
            S.op("dve", lambda e: e.tensor_tensor(out=sp_[:], in0=sp_[:], in1=ep[:], op=ALU.mult), reads=[sp_, ep], writes=[sp_])
            c8 = S.sb("c8", [128, 2, LS], F32)
            c16 = S.sb("c16", [128, 2, LS], F32)
            S.op("dve", lambda e: e.tensor_scalar(out=c8[:], in0=sp_[:], scalar1=-C.LRUC, scalar2=None, op0=ALU.mult), reads=[sp_], writes=[c8])
            S.op("dve", lambda e: e.tensor_scalar(out=c16[:], in0=sp_[:], scalar1=-2.0 * C.LRUC, scalar2=None, op0=ALU.mult), reads=[sp_], writes=[c16])
            xs = [S.sb("lx%d" % i, [128, NTM], F32) for i in range(2)]
            ggs = [S.sb("lgg%d" % i, [128, NTM], BF16) for i in range(2)]
            u = S.sb("lu", [128, NTM], F32)
            ub = S.sb("lub", [128, NTM], BF16)
            rg = S.sb("lr", [128, NTM], F32)
            ig = S.sb("li", [128, NTM], F32)
            av = S.sb("la", [128, NTM], F32)
            bv = S.sb("lb", [128, NTM], F32)
            hd = [S.sb("lh%d" % i, [128, NTM], F32) for i in range(2)]
            ob = [S.sb("lob%d" % i, [128, NTM], BF16) for i in range(2)]
            was = [S.sb("lwa%d" % i, [128, 128], BF16) for i in range(2)]
            wxs = [S.sb("lwx%d" % i, [128, 128], BF16) for i in range(2)]
            segs = [(0, NCTX), (NCTX, NTM)]

            def ldx(c):
                S.load("sp", xs[c % 2], xs[c % 2][:], T["LXT"], T["LXT"].t[c * 128:(c + 1) * 128, :])
                S.load("sp", ggs[c % 2], ggs[c % 2][:], T["GGT"], T["GGT"].t[c * 128:(c + 1) * 128, :])
            ldx(0)
            wi = 0
            for c in range(LS):
                if c + 1 < LS:
                    ldx(c + 1)
                x, gg = xs[c % 2], ggs[c % 2]
                S.op("act", lambda e: e.activation(out=u[:], in_=x[:], func=AF.Identity, scale=lcw[:, 2, c:c + 1], bias=lcb[:, c:c + 1]),
                     reads=[x, lcw, lcb], writes=[u])
                for (a, b) in segs:
                    for tap, sh in ((0, -2), (1, -1), (3, 1)):
                        if sh < 0:
                            oa, ob_, ia, ib = a - sh, b, a, b + sh
                        else:
                            oa, ob_, ia, ib = a, b - sh, a + sh, b
                        S.op("dve", lambda e, tap=tap, oa=oa, ob_=ob_, ia=ia, ib=ib: e.scalar_tensor_tensor(
                            out=u[:, oa:ob_], in0=x[:, ia:ib], scalar=lcw[:, tap, c:c + 1], in1=u[:, oa:ob_], op0=ALU.mult, op1=ALU.add),
                            reads=[x, u, lcw], writes=[u])
                S.op("pool", lambda e: e.tensor_copy(out=ub[:], in_=u[:]), reads=[u], writes=[ub])
                for d in range(2):
                    wa, wx = was[wi % 2], wxs[wi % 2]
                    wi += 1
                    S.load("pool", wa, wa[:], T["lwa"], T["lwa"].t[d, c])
                    S.load("pool", wx, wx[:], T["lwx"], T["lwx"].t[d, c])
                    for (lo, hi) in blocks(0, NTM):
                        w = hi - lo
                        pa = S.psum()
                        S.op("pe", lambda e, pa=pa, lo=lo, hi=hi, w=w: e.matmul(pa[:, :w], lhsT=wa[:], rhs=ub[:, lo:hi], start=True, stop=True), reads=[wa, ub], writes=[pa])
                        px = S.psum()
                        S.op("pe", lambda e, px=px, lo=lo, hi=hi, w=w: e.matmul(px[:, :w], lhsT=wx[:], rhs=ub[:, lo:hi], start=True, stop=True), reads=[wx, ub], writes=[px])
                        S.op("act", lambda e, pa=pa, lo=lo, hi=hi, w=w: e.activation(out=rg[:, lo:hi], in_=pa[:, :w], func=AF.Sigmoid, bias=lba[:, d, c:c + 1]),
                             reads=[pa, lba], writes=[rg])
                        S.op("act", lambda e, px=px, lo=lo, hi=hi, w=w: e.activation(out=ig[:, lo:hi], in_=px[:, :w], func=AF.Sigmoid, bias=lbx[:, d, c:c + 1]),
                             reads=[px, lbx], writes=[ig])
                    S.op("act", lambda e: e.activation(out=av[:], in_=rg[:], func=AF.Exp, scale=c8[:, d, c:c + 1]), reads=[rg, c8], writes=[av])
                    S.op("act", lambda e: e.activation(out=bv[:], in_=rg[:], func=AF.Exp, scale=c16[:, d, c:c + 1]), reads=[rg, c16], writes=[bv])
                    S.op("dve", lambda e: e.tensor_scalar(out=bv[:], in0=bv[:], scalar1=-1.0, scalar2=1.0, op0=ALU.mult, op1=ALU.add), reads=[bv], writes=[bv])
                    S.op("act", lambda e: e.activation(out=bv[:], in_=bv[:], func=AF.Sqrt, bias=tiny[:]), reads=[bv, tiny], writes=[bv])
                    S.op("pool", lambda e: e.tensor_tensor(out=ig[:], in0=ig[:], in1=u[:], op=ALU.mult), reads=[ig, u], writes=[ig])
                    S.op("dve", lambda e: e.tensor_tensor(out=bv[:], in0=bv[:], in1=ig[:], op=ALU.mult), reads=[bv, ig], writes=[bv])
                    H = hd[d]
                    if d == 0:
                        S.op("dve", lambda e: e.tensor_tensor_scan(out=H[:, 0:NCTX], data0=av[:, 0:NCTX], data1=bv[:, 0:NCTX], initial=0.0, op0=ALU.mult, op1=ALU.add),
                             reads=[av, bv], writes=[H])
                        S.op("dve", lambda e: e.tensor_tensor_scan(out=H[:, NCTX:NTM], data0=av[:, NCTX:NTM], data1=bv[:, NCTX:NTM], initial=H[:, NCTX - 1:NCTX],
                                                                   op0=ALU.mult, op1=ALU.add), reads=[av, bv, H], writes=[H])
                    else:
                        S.op("dve", lambda e: e.tensor_tensor_scan(out=rev(H[:, 0:NCTX]), data0=rev(av[:, 0:NCTX]), data1=rev(bv[:, 0:NCTX]), initial=0.0,
                                                                   op0=ALU.mult, op1=ALU.add), reads=[av, bv], writes=[H])
                        S.op("dve", lambda e: e.tensor_tensor_scan(out=rev(H[:, NCTX:NTM]), data0=rev(av[:, NCTX:NTM]), data1=rev(bv[:, NCTX:NTM]), initial=H[:, 0:1],
                                                                   op0=ALU.mult, op1=ALU.add), reads=[av, bv, H], writes=[H])
                S.op("pool", lambda e: e.tensor_tensor(out=hd[0][:], in0=hd[0][:], in1=hd[1][:], op=ALU.add), reads=[hd[0], hd[1]], writes=[hd[0]])
                o = ob[c % 2]
                S.op("dve", lambda e, o=o: e.tensor_tensor(out=o[:], in0=hd[0][:], in1=gg[:], op=ALU.mult), reads=[hd[0], gg], writes=[o])
                S.store("sp", T["BoT"], T["BoT"].t[c * 128:(c + 1) * 128, :], o, o[:])

        with S.scope():
            KQ, KK, R = C.QR // 128, C.KVR // 128, C.ROPE
            cqn = S.sb("cqn", [128, KQ, NTM], BF16)
            S.load("sp", cqn, cqn[:], T["CQN"], T["CQN"].t.rearrange("(k p) n -> p k n", p=128))
            ckv = S.sb("ckv", [128, KK, NTM], BF16)
            S.load("sp", ckv, ckv[:], T["CKVN"], T["CKVN"].t.rearrange("(k p) n -> p k n", p=128))
            krt = S.sb("krt", [R, NTM], BF16)
            S.load("sp", krt, krt[:], T["KRT"], T["KRT"].t)
            cos2 = S.sb("cos2", [R, C.SEQ], F32)
            S.load("sp", cos2, cos2[:], T["cos2"], T["cos2"].t)
            sin2 = S.sb("sin2", [R, C.SEQ], F32)
            S.load("sp", sin2, sin2[:], T["sin2"], T["sin2"].t)
            rotm = S.sb("rotm", [R, R], F32)
            S.load("sp", rotm, rotm[:], T["rotm"], T["rotm"].t)
            NTILE = NTM // 128
            kn = S.sb("kn", [128, NTM], BF16)
            vh = S.sb("vh", [128, NTILE, 128], BF16)
            qn = S.sb("qn", [128, NTM], BF16)
            qr = S.sb("qr", [R, NTM], BF16)
            cos_ = [S.sb("mo%d" % i, [128, NTM], BF16) for i in range(2)]
            wqs = [S.sb("wq%d" % i, [128, KQ, C.QKD], BF16) for i in range(2)]
            wks = [S.sb("wk%d" % i, [128, KK, C.NOPE + C.VD], BF16) for i in range(2)]
            pts = [S.sb("mpt%d" % i, [128, 512], BF16) for i in range(4)]
            r32 = [S.sb("mr%d" % i, [R, 512], F32) for i in range(2)]
            t32 = [S.sb("mt%d" % i, [R, 512], F32) for i in range(2)]
            rl = S.sb("mrl", [128, 512], F32)
            mscale = float(C.QKD) ** -0.5
            wqv = T["wq"].t.rearrange("(k p) c -> p k c", p=128)
            wkv = T["wkv"].t.rearrange("(k p) c -> p k c", p=128)

            def ldwm(hm):
                S.load("pool", wqs[hm % 2], wqs[hm % 2][:], T["wq"], wqv[:, :, hm * C.QKD:(hm + 1) * C.QKD])
                S.load("pool", wks[hm % 2], wks[hm % 2][:], T["wkv"], wkv[:, :, hm * (C.NOPE + C.VD):(hm + 1) * (C.NOPE + C.VD)])
            ldwm(0)
            it = 0
            for hm in range(MS):
                if hm + 1 < MS:
                    ldwm(hm + 1)
                wq, wk = wqs[hm % 2], wks[hm % 2]
                co = cos_[hm % 2]
                for bi, (lo, hi) in enumerate(blocks(0, NCTX) + blocks(NCTX, NTM)):
                    w = hi - lo
                    p = S.psum()
                    for k in range(KK):
                        S.op("pe", lambda e, k=k, p=p, lo=lo, hi=hi, w=w: e.matmul(p[:, :w], lhsT=wk[:, k, 0:C.NOPE], rhs=ckv[:, k, lo:hi], start=(k == 0), stop=(k == KK - 1)),
                             reads=[wk, ckv], writes=[p], inc=(k == KK - 1))
                    S.op("act", lambda e, p=p, lo=lo, hi=hi, w=w: e.copy(out=kn[:, lo:hi], in_=p[:, :w]), reads=[p], writes=[kn])
                    p = S.psum()
                    for k in range(KQ):
                        S.op("pe", lambda e, k=k, p=p, lo=lo, hi=hi, w=w: e.matmul(p[:, :w], lhsT=wq[:, k, 0:C.NOPE], rhs=cqn[:, k, lo:hi], start=(k == 0), stop=(k == KQ - 1)),
                             reads=[wq, cqn], writes=[p], inc=(k == KQ - 1))
                    S.op("dve", lambda e, p=p, lo=lo, hi=hi, w=w: e.tensor_copy(out=qn[:, lo:hi], in_=p[:, :w]), reads=[p], writes=[qn])
                    p = S.psum()
                    for k in range(KQ):
                        S.op("pe", lambda e, k=k, p=p, lo=lo, hi=hi, w=w: e.matmul(p[:R, :w], lhsT=wq[:, k, C.NOPE:C.QKD], rhs=cqn[:, k, lo:hi], start=(k == 0), stop=(k == KQ - 1)),
                             reads=[wq, cqn], writes=[p], inc=(k == KQ - 1))
                    if lo < NCTX:
                        S.op("act", lambda e, p=p, lo=lo, hi=hi, w=w: e.copy(out=qr[:, lo:hi], in_=p[:R, :w]), reads=[p], writes=[qr])
                    else:
                        raw, t1 = r32[bi % 2], t32[bi % 2]
                        pc = slice(lo - NCTX, hi - NCTX)
                        S.op("act", lambda e, p=p, raw=raw, w=w: e.copy(out=raw[:, :w], in_=p[:R, :w]), reads=[p], writes=[raw])
                        p2 = S.psum()
                        S.op("pe", lambda e, p2=p2, raw=raw, w=w: e.matmul(p2[:R, :w], lhsT=rotm[:], rhs=raw[:, :w], start=True, stop=True), reads=[rotm, raw], writes=[p2])
                        S.op("dve", lambda e, raw=raw, t1=t1, w=w, pc=pc: e.tensor_tensor(out=t1[:, :w], in0=raw[:, :w], in1=cos2[:, pc], op=ALU.mult), reads=[raw, cos2], writes=[t1])
                        S.op("dve", lambda e, raw=raw, p2=p2, w=w, pc=pc: e.tensor_tensor(out=raw[:, :w], in0=p2[:R, :w], in1=sin2[:, pc], op=ALU.mult), reads=[p2, sin2], writes=[raw])
                        S.op("dve", lambda e, raw=raw, t1=t1, lo=lo, hi=hi, w=w: e.tensor_tensor(out=qr[:, lo:hi], in0=raw[:, :w], in1=t1[:, :w], op=ALU.add), reads=[raw, t1], writes=[qr])
                for t0 in range(0, NTILE, 4):
                    nt = min(4, NTILE - t0)
                    p = S.psum()
                    for ti in range(nt):
                        for k in range(KK):
                            S.op("pe", lambda e, k=k, p=p, ti=ti, t0=t0: e.matmul(p[:, ti * 128:(ti + 1) * 128], lhsT=ckv[:, k, (t0 + ti) * 128:(t0 + ti + 1) * 128],
                                                                               rhs=wk[:, k, C.NOPE:C.NOPE + C.VD], start=(k == 0), stop=(k == KK - 1)),
                                 reads=[wk, ckv], writes=[p], inc=(ti == nt - 1 and k == KK - 1))
                    S.op("act", lambda e, p=p, t0=t0, nt=nt: e.copy(out=vh[:, t0:t0 + nt, :], in_=p[:, :nt * 128].rearrange("p (t c) -> p t c", c=128)), reads=[p], writes=[vh])
                qblocks = [(lo, hi, list(range(NCC))) for (lo, hi) in blocks(0, NCTX)] + [(lo, hi, list(range(NTILE))) for (lo, hi) in blocks(NCTX, NTM)]
                for (lo, hi, chunks) in qblocks:
                    w = hi - lo
                    po = S.psum()
                    pl = S.psum(avoid=[po])
                    nck = len(chunks)
                    prev = None

                    def pvm(ci, kc, pt):
                        S.op("pe", lambda e: e.matmul(po[:, :w], lhsT=vh[:, kc, :], rhs=pt[:, :w], start=(ci == 0), stop=(ci == nck - 1)),
                             reads=[vh, pt], writes=[po], inc=(ci == nck - 1))
                        S.op("pe", lambda e: e.matmul(pl[:, :w], lhsT=ones[:], rhs=pt[:, :w], start=(ci == 0), stop=(ci == nck - 1)),
                             reads=[ones, pt], writes=[pl], inc=(ci == nck - 1))
                    for ci, kc in enumerate(chunks):
                        ps = S.psum(avoid=[po, pl])
                        S.op("pe", lambda e, ps=ps, kc=kc: e.matmul(ps[:, :w], lhsT=kn[:, kc * 128:(kc + 1) * 128], rhs=qn[:, lo:hi], start=True, stop=False),
                             reads=[kn, qn], writes=[ps], inc=False)
                        S.op("pe", lambda e, ps=ps, kc=kc: e.matmul(ps[:, :w], lhsT=krt[:, kc * 128:(kc + 1) * 128], rhs=qr[:, lo:hi], start=False, stop=True),
                             reads=[krt, qr], writes=[ps], inc=True)
                        pt = pts[it % 4]
                        it += 1
                        S.op("act", lambda e, ps=ps, pt=pt: e.activation(out=pt[:, :w], in_=ps[:, :w], func=AF.Exp, scale=mscale), reads=[ps], writes=[pt])
                        if prev is not None:
                            pvm(*prev)
                        prev = (ci, kc, pt)
                    pvm(*prev)
                    S.op("dve", lambda e: e.reciprocal(out=rl[:, :w], in_=pl[:, :w]), reads=[pl], writes=[rl])
                    S.op("dve", lambda e: e.tensor_tensor(out=co[:, lo:hi], in0=po[:, :w], in1=rl[:, :w], op=ALU.mult), reads=[po, rl], writes=[co])
                S.store("sp", T["CoT"], T["CoT"].t[hm * C.VD:(hm + 1) * C.VD, :], co, co[:])


def na_tables(C, rpb_l):
    cidx = np.arange(C.GW)
    col_idx = np.clip(cidx[None, :] - cidx[:, None], -(C.WINW - 1), C.WINW - 1) + (C.WINW - 1)
    G = np.asarray(rpb_l, np.float32)[:, :, col_idx]
    Gt = G.transpose(0, 1, 3, 2)
    npair = 2 * C.WINH - 2
    pairs = np.concatenate([Gt[:, :npair], Gt[:, 1:npair + 1]], axis=2)
    bt = np.ascontiguousarray(pairs.transpose(2, 0, 1, 3))
    c_start = np.clip(cidx - C.WINW // 2, 0, C.GW - C.WINW)
    in_win = (cidx[None, :] >= c_start[:, None]) & (cidx[None, :] < c_start[:, None] + C.WINW)
    m = np.where(in_win.T, 0.0, -1e30).astype(np.float32)
    mask = np.ascontiguousarray(np.concatenate([m, m], 0))
    return bt, mask


_PROGS = {}


def get_prog(C, kind, **flags):
    key = (kind, tuple(sorted(flags.items())))
    if key not in _PROGS:
        _PROGS[key] = build_prog(C, kind, **flags)
    return _PROGS[key]


def default_runner(nc, maps):
    return run_bass_kernel_spmd(nc, maps, core_ids=list(range(len(maps)))).results


def run_module(C, I, runner=default_runner, dbg=None):
    NB, NOWN, NCTX, NC, KD = C.NB, C.NOWN, C.NCTX, C.NC, C.KD
    HS, LS, MS = C.NAH // 2, C.LRUB // 2, C.MH // 2
    f32 = lambda a: np.ascontiguousarray(np.asarray(a, np.float32))
    cores = [(b, s) for b in range(NB) for s in range(2)]
    XT = [f32(np.asarray(I["x"][b]).T) for b in range(NB)]
    XC = [f32(np.asarray(I["ctx"][b]).T) for b in range(NB)]
    cos2, sin2, rotm = rope_tables(C)
    ident = np.eye(128, dtype=np.float32)

    def halo_cols(full, b, s, nrows, dt):
        o = np.zeros((nrows, NC), dt)
        o[:, :NCTX] = full[:, :NCTX]
        o[:, C.OWN0:C.OWN0 + NOWN] = full[:, NCTX + s * NOWN:NCTX + (s + 1) * NOWN]
        if s == 1:
            o[:, NCTX + 1] = full[:, NCTX + NOWN - 1]
        else:
            o[:, NC - 1] = full[:, NCTX + NOWN]
        return o

    modv = [None] * len(cores)
    ABC = None
    out_final = None
    for l in range(C.DEPTH + 1):
        merge = l > 0
        inproj = l < C.DEPTH
        final = l == C.DEPTH
        nc, ins, outs, _ = get_prog(C, "R", merge=merge, inproj=inproj, final=final)
        maps = []
        for ci, (b, s) in enumerate(cores):
            full = np.concatenate([XC[b], XT[b]], 1)
            m = {"xT": halo_cols(full, b, s, C.D, np.float32)}
            cf = np.zeros((128, 3), np.float32)
            cf[:, 1] = float(s == 1)
            cf[:, 2] = float(s == 0)
            m["cflag"] = cf
            if merge:
                p = l - 1
                m["modp"] = modv[ci]
                for n in ("AT", "BT", "CT"):
                    m[n] = halo_cols(ABC[n][b], b, s, C.BW, NPBF)
                m["nmix_p"] = pp(I["norm_mix"][p])
                m["nffn_p"] = pp(I["norm_ffn"][p])
                m["win_p"] = f32(I["w_in"][p])
                m["wbr_p"] = f32(I["w_branch"][p])
                m["wout_p"] = f32(I["w_out"][p])
                m["wup_p"] = f32(I["ffn_w_up"][p])
                cw = np.asarray(I["ffn_conv_w"][p], np.float32)
                m["cw_p"] = np.ascontiguousarray(cw.reshape(3, -1, 128).transpose(2, 0, 1))
                m["cb_p"] = pp(I["ffn_conv_b"][p])
                m["wdn_p"] = f32(I["ffn_w_down"][p])
            if inproj:
                m["cT"] = np.ascontiguousarray(np.stack([pp(I["c"][b]), pp(I["c_ctx"])], -1))
                m["wmod"] = f32(I["w_mod"][l])
                m["bmod"] = pp(I["b_mod"][l])
                m["nmix"] = pp(I["norm_mix"][l])
                m["win"] = f32(I["w_in"][l])
                m["qn"] = pp(I["mla_q_norm"][l])
                m["kvn"] = pp(I["mla_kv_norm"][l])
                m["cos2"] = np.ascontiguousarray(cos2[:, s * NOWN:(s + 1) * NOWN])
                m["sin2"] = np.ascontiguousarray(sin2[:, s * NOWN:(s + 1) * NOWN])
                m["rotm"] = rotm
            if final:
                m["nfin"] = pp(I["norm_final"])
            maps.append(m)
        res = runner(nc, maps)
        if dbg is not None:
            dbg.append(("R", l, res))
        if merge:
            for ci, (b, s) in enumerate(cores):
                xo = res[ci]["xTo"]
                XT[b][:, s * NOWN:(s + 1) * NOWN] = xo[:, C.OWN0:C.OWN0 + NOWN]
                if s == 0:
                    XC[b] = np.ascontiguousarray(xo[:, :NCTX])
        if final:
            out = np.zeros((NB, C.SEQ, C.D), np.float32)
            for ci, (b, s) in enumerate(cores):
                out[b, s * NOWN:(s + 1) * NOWN, :] = res[ci]["oT"].T
            out_final = out
            break
        def fullcat(name, b, axis):
            r0, r1 = res[2 * b], res[2 * b + 1]
            if axis == 1:
                return np.concatenate([r0[name][:, :NCTX], r0[name][:, NCTX:], r1[name][:, NCTX:]], 1)
            return np.concatenate([r0[name][:NCTX], r0[name][NCTX:], r1[name][NCTX:]], 0)
        for ci in range(len(cores)):
            modv[ci] = res[ci]["modo"]
        bt, mask = na_tables(C, I["na_rpb"][l])
        ncm, insm, outsm, _ = get_prog(C, "M")
        mmaps = []
        for ci, (b, s) in enumerate(cores):
            m = {"ident": ident, "mask": mask}
            m["QT"] = np.ascontiguousarray(fullcat("QT", b, 1)[s * HS * 128:(s + 1) * HS * 128])
            m["KT"] = np.ascontiguousarray(fullcat("KT", b, 1)[s * HS * 128:(s + 1) * HS * 128])
            m["V"] = np.ascontiguousarray(fullcat("V", b, 0)[:, s * HS * 128:(s + 1) * HS * 128])
            m["BTin"] = np.ascontiguousarray(bt[:, s * HS:(s + 1) * HS])
            m["LXT"] = np.ascontiguousarray(fullcat("LXT", b, 1)[s * LS * 128:(s + 1) * LS * 128])
            m["GGT"] = np.ascontiguousarray(fullcat("GGT", b, 1)[s * LS * 128:(s + 1) * LS * 128])
            lsl = slice(s * LS * 128, (s + 1) * LS * 128)
            cw = np.asarray(I["lru_conv_w"][l], np.float32)[:, lsl]
            m["lcw"] = np.ascontiguousarray(cw.reshape(4, LS, 128).transpose(2, 0, 1))
            m["lcb"] = pp(np.asarray(I["lru_conv_b"][l])[lsl])
            m["lwa"] = f32(np.asarray(I["lru_w_a"][l])[:, s * LS:(s + 1) * LS])
            m["lwx"] = f32(np.asarray(I["lru_w_x"][l])[:, s * LS:(s + 1) * LS])
            for nm_, src in (("lba", "lru_b_a"), ("lbx", "lru_b_x"), ("llam", "lru_lam")):
                v = np.asarray(I[src][l], np.float32)[:, lsl]
                m[nm_] = np.ascontiguousarray(v.reshape(2, LS, 128).transpose(2, 0, 1))
            m["CQN"] = fullcat("CQN", b, 1)
            m["CKVN"] = fullcat("CKVN", b, 1)
            m["KRT"] = fullcat("KRT", b, 1)
            m["wq"] = f32(np.asarray(I["mla_w_q_up"][l])[:, s * MS * C.QKD:(s + 1) * MS * C.QKD])
            kvw = C.NOPE + C.VD
            m["wkv"] = f32(np.asarray(I["mla_w_kv_up"][l])[:, s * MS * kvw:(s + 1) * MS * kvw])
            m["cos2"], m["sin2"], m["rotm"] = cos2, sin2, rotm
            mmaps.append(m)
        mres = runner(ncm, mmaps)
        if dbg is not None:
            dbg.append(("M", l, mres))
        ABC = {}
        for n, src in (("AT", "AT"), ("BT", "BoT"), ("CT", "CoT")):
            ABC[n] = [np.concatenate([mres[2 * b][src], mres[2 * b + 1][src]], 0) for b in range(NB)]
    return out_final


def kernel(**inputs):
    C = Cfg()
    return run_module(C, inputs)
```
